# Optimizing a Trainium2 kernel written in Bass

```python
import jax, jax.numpy as jnp
from jax import lax
import numpy as np

D_MODEL = 1024
BATCH = 8
SEQ = 4096
DEPTH = 2

CTX_LEN = 256
GRID_W = 64
W_LRU = 1024
LRU_HEADS = 16
LRU_HEAD_DIM = W_LRU // LRU_HEADS
CONV_W = 4
LRU_C = 8.0
W_FFT = 512
FFT_GROUPS = 4
FFT_GROUP_DIM = W_FFT // FFT_GROUPS
W_POOL = 512
POOL_WINDOWS = (2, 4, 8, 16)
POOL_GROUPS = len(POOL_WINDOWS)
POOL_GROUP_DIM = W_POOL // POOL_GROUPS
N_BRANCH = 3
IN_COLS = 2 * W_LRU + 2 * W_FFT + 2 * W_POOL + N_BRANCH * D_MODEL
SPLIT_POINTS = tuple(int(v) for v in np.cumsum([W_LRU, W_LRU, W_FFT, W_FFT, W_POOL, W_POOL]))
RMS_EPS = 1e-6
POS_BASE = 10000.0

kernel_name = "hybrid_rglru_fourier_pool_dit"


def rms_norm(x, g):
    xf = x.astype(jnp.float32)
    y = xf * lax.rsqrt(jnp.mean(xf * xf, axis=-1, keepdims=True) + RMS_EPS)
    return (y * g.astype(jnp.float32)).astype(x.dtype)


def sincos_2d(n_tokens, d, dtype):
    rows = n_tokens // GRID_W
    gr, gc = jnp.meshgrid(jnp.arange(rows), jnp.arange(GRID_W), indexing="ij")
    quarter = d // 4
    omega = 1.0 / (POS_BASE ** (jnp.arange(quarter, dtype=jnp.float32) / quarter))

    def emb(p):
        ang = p.reshape(-1).astype(jnp.float32)[:, None] * omega[None, :]
        return jnp.concatenate([jnp.sin(ang), jnp.cos(ang)], axis=-1)

    return jnp.concatenate([emb(gr), emb(gc)], axis=-1).astype(dtype)


def in_split(h, w_in):
    z = h @ w_in
    u_lru, z_lru, u_fft, z_fft, u_pool, z_pool, gate_logits = jnp.split(z, SPLIT_POINTS, axis=-1)
    return u_lru, z_lru, u_fft, z_fft, u_pool, z_pool, gate_logits


def depthwise_conv(u, w, b):
    pad = CONV_W // 2
    out = lax.conv_general_dilated(
        u, w[:, None, :], window_strides=(1,), padding=[(pad, CONV_W - 1 - pad)],
        dimension_numbers=("NWC", "WIO", "NWC"), feature_group_count=u.shape[-1])
    return out + b


def block_diag(u, w, b):
    bsz, t, _ = u.shape
    uh = u.reshape(bsz, t, LRU_HEADS, LRU_HEAD_DIM)
    return jnp.einsum("bthi,hij->bthj", uh, w).reshape(bsz, t, W_LRU) + b


def lru_coeffs(xc, wa, ba, wx, bx, lam):
    r = jax.nn.sigmoid(block_diag(xc, wa, ba)).astype(jnp.float32)
    i = jax.nn.sigmoid(block_diag(xc, wx, bx)).astype(jnp.float32)
    log_a = -LRU_C * r * jax.nn.softplus(-lam.astype(jnp.float32))
    a = jnp.exp(log_a)
    b = jnp.sqrt(-jnp.expm1(2.0 * log_a)) * (i * xc.astype(jnp.float32))
    return a, b


def linear_scan(a, b, h0):
    def combine(lft, rgt):
        return lft[0] * rgt[0], rgt[0] * lft[1] + rgt[1]
    acc_a, acc_h = lax.associative_scan(combine, (a, b), axis=1)
    return acc_h + acc_a * h0[:, None, :]


def lru_direction(xc_ctx, xc_lat, wa, ba, wx, bx, lam, reverse):
    flip = (lambda t: jnp.flip(t, axis=1)) if reverse else (lambda t: t)
    a_c, b_c = lru_coeffs(flip(xc_ctx), wa, ba, wx, bx, lam)
    h_c = linear_scan(a_c, b_c, jnp.zeros((xc_ctx.shape[0], W_LRU), jnp.float32))
    a_l, b_l = lru_coeffs(flip(xc_lat), wa, ba, wx, bx, lam)
    h_l = linear_scan(a_l, b_l, h_c[:, -1])
    return flip(h_c), flip(h_l)


def fourier_mix(u, w):
    bsz, t, _ = u.shape
    ug = u.reshape(bsz, t, FFT_GROUPS, FFT_GROUP_DIM).astype(jnp.float32)
    f = jnp.fft.fft2(ug, axes=(1, 3), norm="ortho").real.astype(u.dtype)
    return jnp.einsum("btgi,gij->btgj", f, w).reshape(bsz, t, W_FFT)


def pool_mix(u, w, scale):
    bsz, t, _ = u.shape
    uf = u.astype(jnp.float32)
    cs = jnp.concatenate([jnp.zeros_like(uf[:, :1]), jnp.cumsum(uf, axis=1)], axis=1)
    pos = jnp.arange(t)
    parts = []
    for g, win in enumerate(POOL_WINDOWS):
        sl = slice(g * POOL_GROUP_DIM, (g + 1) * POOL_GROUP_DIM)
        lo = jnp.clip(pos - win // 2, 0, t)
        hi = jnp.clip(pos + win - win // 2, 0, t)
        csg = cs[..., sl]
        cnt = (hi - lo).astype(jnp.float32)[None, :, None]
        mean = (jnp.take(csg, hi, axis=1) - jnp.take(csg, lo, axis=1)) / cnt
        parts.append(mean - uf[..., sl])
    p = jnp.concatenate(parts, axis=-1).astype(u.dtype).reshape(bsz, t, POOL_GROUPS, POOL_GROUP_DIM)
    y = jnp.einsum("btgi,gij->btgj", p, w).reshape(bsz, t, W_POOL)
    return y * scale


def branch_merge(y_lru, z_lru, y_fft, z_fft, y_pool, z_pool, gate_logits, proj_a, proj_b, proj_c, w_out):
    ya = (y_lru * jax.nn.silu(z_lru)) @ proj_a
    yb = (y_fft * jax.nn.silu(z_fft)) @ proj_b
    yc = (y_pool * jax.nn.silu(z_pool)) @ proj_c
    g = jax.nn.sigmoid(gate_logits.reshape(gate_logits.shape[:-1] + (N_BRANCH, D_MODEL)))
    m = g[..., 0, :] * ya + g[..., 1, :] * yb + g[..., 2, :] * yc
    return m @ w_out


def setup_inputs(seed: int = 0) -> dict:
    key = jax.random.key(seed)
    ks = jax.random.split(key, 24)
    f32 = jnp.float32
    nrm = lambda k, shape, s: jax.random.normal(k, shape, f32) * s
    u = jax.random.uniform(ks[14], (DEPTH, 2, W_LRU), f32, minval=0.9, maxval=0.999)
    s = u ** (1.0 / LRU_C)
    lam = jnp.log(s) - jnp.log1p(-s)
    return {
        "x": nrm(ks[0], (BATCH, SEQ, D_MODEL), 1.0),
        "c": nrm(ks[1], (BATCH, D_MODEL), 1.0),
        "ctx": nrm(ks[2], (BATCH, CTX_LEN, D_MODEL), 1.0),
        "c_ctx": nrm(ks[3], (D_MODEL,), 1.0),
        "norm_g": 1.0 + nrm(ks[4], (DEPTH, D_MODEL), 0.02),
        "ada_w": nrm(ks[5], (DEPTH, D_MODEL, 3 * D_MODEL), 0.5 * D_MODEL ** -0.5),
        "ada_b": nrm(ks[6], (DEPTH, 3 * D_MODEL), 0.01),
        "w_in": nrm(ks[7], (DEPTH, D_MODEL, IN_COLS), D_MODEL ** -0.5),
        "conv_w": nrm(ks[8], (DEPTH, CONV_W, W_LRU), CONV_W ** -0.5),
        "conv_b": nrm(ks[9], (DEPTH, W_LRU), 0.01),
        "lru_wa": nrm(ks[10], (DEPTH, 2, LRU_HEADS, LRU_HEAD_DIM, LRU_HEAD_DIM), LRU_HEAD_DIM ** -0.5),
        "lru_ba": nrm(ks[11], (DEPTH, 2, W_LRU), 0.01),
        "lru_wx": nrm(ks[12], (DEPTH, 2, LRU_HEADS, LRU_HEAD_DIM, LRU_HEAD_DIM), LRU_HEAD_DIM ** -0.5),
        "lru_bx": nrm(ks[13], (DEPTH, 2, W_LRU), 0.01),
        "lru_lam": lam,
        "fft_w": nrm(ks[15], (DEPTH, FFT_GROUPS, FFT_GROUP_DIM, FFT_GROUP_DIM), FFT_GROUP_DIM ** -0.5),
        "pool_w": nrm(ks[16], (DEPTH, POOL_GROUPS, POOL_GROUP_DIM, POOL_GROUP_DIM), POOL_GROUP_DIM ** -0.5),
        "pool_scale": 1.0 + nrm(ks[17], (DEPTH, W_POOL), 0.02),
        "proj_a": nrm(ks[18], (DEPTH, W_LRU, D_MODEL), W_LRU ** -0.5),
        "proj_b": nrm(ks[19], (DEPTH, W_FFT, D_MODEL), W_FFT ** -0.5),
        "proj_c": nrm(ks[20], (DEPTH, W_POOL, D_MODEL), W_POOL ** -0.5),
        "w_out": nrm(ks[21], (DEPTH, D_MODEL, D_MODEL), D_MODEL ** -0.5),
        "final_g": 1.0 + nrm(ks[22], (D_MODEL,), 0.02),
    }


def reference(x, c, ctx, c_ctx, norm_g, ada_w, ada_b, w_in, conv_w, conv_b, lru_wa, lru_ba, lru_wx, lru_bx,
              lru_lam, fft_w, pool_w, pool_scale, proj_a, proj_b, proj_c, w_out, final_g):
    n_lat = x.shape[1]
    x = x + sincos_2d(n_lat, x.shape[-1], x.dtype)[None]
    for l in range(DEPTH):
        last = l == DEPTH - 1
        mod_lat = jax.nn.silu(c) @ ada_w[l] + ada_b[l]
        mod_ctx = jax.nn.silu(c_ctx) @ ada_w[l] + ada_b[l]
        sh_l, sc_l, gt_l = jnp.split(mod_lat, 3, axis=-1)
        sh_c, sc_c, gt_c = jnp.split(mod_ctx, 3, axis=-1)
        h_lat = rms_norm(x, norm_g[l]) * (1.0 + sc_l[:, None]) + sh_l[:, None]
        h_ctx = rms_norm(ctx, norm_g[l]) * (1.0 + sc_c) + sh_c
        ul_lru, zl_lru, ul_fft, zl_fft, ul_pool, zl_pool, gl = in_split(h_lat, w_in[l])
        uc_lru, zc_lru, uc_fft, zc_fft, uc_pool, zc_pool, gc = in_split(h_ctx, w_in[l])
        xc_lat = depthwise_conv(ul_lru, conv_w[l], conv_b[l])
        xc_ctx = depthwise_conv(uc_lru, conv_w[l], conv_b[l])
        hc_f, hl_f = lru_direction(xc_ctx, xc_lat, lru_wa[l, 0], lru_ba[l, 0], lru_wx[l, 0], lru_bx[l, 0],
                                   lru_lam[l, 0], False)
        hc_b, hl_b = lru_direction(xc_ctx, xc_lat, lru_wa[l, 1], lru_ba[l, 1], lru_wx[l, 1], lru_bx[l, 1],
                                   lru_lam[l, 1], True)
        y_lru_lat = (hl_f + hl_b).astype(x.dtype)
        out_lat = branch_merge(y_lru_lat, zl_lru, fourier_mix(ul_fft, fft_w[l]), zl_fft,
                               pool_mix(ul_pool, pool_w[l], pool_scale[l]), zl_pool, gl,
                               proj_a[l], proj_b[l], proj_c[l], w_out[l])
        if not last:
            y_lru_ctx = (hc_f + hc_b).astype(ctx.dtype)
            out_ctx = branch_merge(y_lru_ctx, zc_lru, fourier_mix(uc_fft, fft_w[l]), zc_fft,
                                   pool_mix(uc_pool, pool_w[l], pool_scale[l]), zc_pool, gc,
                                   proj_a[l], proj_b[l], proj_c[l], w_out[l])
            ctx = ctx + gt_c * out_ctx
        x = x + gt_l[:, None] * out_lat
    return rms_norm(x, final_g)
```

```python
from contextlib import ExitStack

import ml_dtypes
import numpy as np

import concourse.bass as bass
import concourse.mybir as mybir
from concourse.bass_utils import run_bass_kernel_spmd

F32 = mybir.dt.float32
BF16 = mybir.dt.bfloat16
AF = mybir.ActivationFunctionType
ALU = mybir.AluOpType

D = 1024
SEQ = 4096
CTX = 256
DEPTH = 2
NT = 4400
C0 = 16
L0 = 288
TILES = [(C0, 256)] + [(L0 + 512 * i, 512) for i in range(8)]
NTI = len(TILES)
INC = 7168
EPS = 1e-6
V_NG, V_AB, V_CW, V_CB, V_BA, V_BX, V_LAM, V_PS = 0, 8, 32, 64, 72, 88, 104, 120
V_L = 124
V_FG = 2 * V_L
NV = V_FG + 8


class Sched:
    def __init__(self, nc, stack):
        self.nc = nc
        self.stack = stack
        self.engs = {}
        for name, e in (("pe", nc.tensor), ("act", nc.scalar), ("dve", nc.vector),
                        ("pool", nc.gpsimd), ("sp", nc.sync)):
            sem = stack.enter_context(nc.semaphore("sem_" + name))
            self.engs[name] = dict(e=e, sem=sem, cnt=0, seen={}, name=name)
        self.dsems = {}
        self.lastw = {}
        self.reads = {}

    def dsem(self, key):
        if key not in self.dsems:
            sem = self.stack.enter_context(self.nc.semaphore("dsem_%d" % len(self.dsems)))
            self.dsems[key] = dict(sem=sem, cnt=0)
        return self.dsems[key]

    def _deps(self, reads, writes):
        deps = []
        for r in reads:
            if r in self.lastw:
                deps.append(self.lastw[r])
        for w in writes:
            if w in self.lastw:
                deps.append(self.lastw[w])
            deps.extend(self.reads.get(w, ()))
        return deps

    def _wait(self, E, deps, skip_self=False):
        best = {}
        for (sem, val, owner) in deps:
            if skip_self and owner == E["name"]:
                continue
            k = id(sem)
            if k not in best or best[k][1] < val:
                best[k] = (sem, val)
        for k, (sem, val) in best.items():
            if E["seen"].get(k, 0) >= val:
                continue
            E["e"].wait_ge(sem, val)
            E["seen"][k] = val

    def _record(self, tok, reads, writes):
        for w in writes:
            self.lastw[w] = tok
            self.reads[w] = []
        for r in reads:
            if r in writes:
                continue
            self.reads.setdefault(r, []).append(tok)

    def op(self, eng, fn, reads=(), writes=(), inc=True):
        E = self.engs[eng]
        self._wait(E, self._deps(reads, writes), skip_self=(eng == "pe"))
        ins = fn()
        if inc:
            ins.then_inc(E["sem"], 1)
            E["cnt"] += 1
            tok = (E["sem"], E["cnt"], eng)
        else:
            tok = (E["sem"], E["cnt"] + 1, eng)
        self._record(tok, reads, writes)
        return ins

    def dma(self, q, out, in_, reads=(), writes=(), key=None, **kw):
        E = self.engs[q]
        self._wait(E, self._deps(reads, writes))
        Dm = self.dsem(key)
        ins = E["e"].dma_start(out=out, in_=in_, **kw)
        ins.then_inc(Dm["sem"], 16)
        Dm["cnt"] += 16
        self._record((Dm["sem"], Dm["cnt"], "dma"), reads, writes)
        return ins

    def barrier(self):
        toks = [(E["sem"], E["cnt"], n) for n, E in self.engs.items() if E["cnt"] > 0]
        toks += [(Dm["sem"], Dm["cnt"], "dma") for Dm in self.dsems.values() if Dm["cnt"] > 0]
        for n, E in self.engs.items():
            self._wait(E, [t for t in toks if t[2] != n])
        self.lastw = {}
        self.reads = {}


def build_program(debug=False):
    nc = bass.Bass("TRN2", target_bir_lowering=False)

    def din(name, shape, dt=F32):
        return nc.dram_tensor(name, list(shape), dt, kind="ExternalInput").ap()

    xT = din("xT", [D, SEQ])
    ctxT = din("ctxT", [D, CTX])
    posT = din("posT", [D, SEQ])
    cc = din("cc", [128, 16])
    vecs_d = din("vecs", [128, NV])
    ada_w = din("ada_w", [DEPTH, D, 3 * D])
    w_in = din("w_in", [DEPTH, D, INC])
    bd_d = din("bd", [DEPTH, 8, 128, 4, 128])
    fftw_d = din("fft_w", [DEPTH, 4, 128, 128])
    poolw_d = din("pool_w", [DEPTH, 4, 128, 128])
    proj_a = din("proj_a", [DEPTH, D, D])
    proj_b = din("proj_b", [DEPTH, 512, D])
    proj_c = din("proj_c", [DEPTH, 512, D])
    w_out = din("w_out", [DEPTH, D, D])
    ident_d = din("ident", [128, 128])
    edge_d = din("edge", [128, 64])
    cs128_d = din("cs128", [128, 256], BF16)
    cs256_d = din("cs256", [128, 2, 2, 256], BF16)
    dft_d = din("dft", [4, 2, 2, 128, 8, 512], BF16)
    cs4_d = din("cs4", [128, 4, 4, 256], BF16)
    okind = "ExternalOutput" if debug else "Internal"
    XS = [nc.dram_tensor("xs%d" % l, [D, NT], F32, kind=okind).ap() for l in range(DEPTH)]
    GT = [nc.dram_tensor("gt%d" % l, [2048, NT], BF16, kind=okind).ap() for l in range(DEPTH)]
    outT = nc.dram_tensor("outT", [D, SEQ], F32, kind="ExternalOutput").ap()
    WBS = [nc.dram_tensor("wbs%d" % l, [3072, D], BF16, kind="Internal").ap() for l in range(DEPTH)]
    WGS = [nc.dram_tensor("wgs%d" % l, [D, 3 * D], BF16, kind="Internal").ap() for l in range(DEPTH)]

    with ExitStack() as st:
        S = Sched(nc, st)

        uid = [0]

        def sb(stack, name, shape, dt=F32):
            uid[0] += 1
            return stack.enter_context(nc.sbuf_tensor("%s_%d" % (name, uid[0]), list(shape), dt))

        PS = [st.enter_context(nc.psum_tensor("ps%d" % i, [128, 512], F32)) for i in range(8)]
        bank_rr = [0]

        def bank():
            b = bank_rr[0]
            bank_rr[0] = (b + 1) % 8
            return b

        def mm(b, n, pairs, reads, c_off=0):
            k = len(pairs)
            for i, (l_, r_) in enumerate(pairs):
                S.op("pe", lambda l_=l_, r_=r_, i=i: nc.tensor.matmul(
                    PS[b][:, c_off:c_off + n], lhsT=l_, rhs=r_, start=(i == 0), stop=(i == k - 1)),
                    reads=(reads if i == 0 else ()), writes=[("ps", b)], inc=(i == k - 1))

        VEC = sb(st, "vec", [128, NV])
        CCs = sb(st, "ccs", [128, 8, 2])
        MOD = sb(st, "mod", [128, DEPTH, 24, 2])
        GM = sb(st, "gm", [128, DEPTH, 8, 2])
        CNEG = sb(st, "cneg", [128, DEPTH, 16])
        CNEG2 = sb(st, "cneg2", [128, DEPTH, 16])
        ONES = sb(st, "ones", [128, 128], BF16)
        IDENT = sb(st, "identt", [128, 128])
        EDGE = sb(st, "edget", [128, 64])
        CS128 = sb(st, "cs128t", [128, 256], BF16)
        CS256 = sb(st, "cs256t", [128, 2, 2, 256], BF16)

        S.dma("sp", VEC[:], vecs_d, writes=["VEC"], key="c0")
        S.dma("sp", IDENT[:], ident_d, writes=["IDENT"], key="c1")
        S.dma("sp", EDGE[:], edge_d, writes=["EDGE"], key="c2")
        S.dma("sp", CS128[:], cs128_d, writes=["CS128"], key="c3")
        S.dma("sp", CS256[:], cs256_d, writes=["CS256"], key="c4")
        S.op("dve", lambda: nc.vector.memset(ONES[:], 1.0), writes=["ONES"])

        def vcol(l, base, i):
            c = l * V_L + base + i
            return VEC[:, c:c + 1]

        ada_piece_bufs = [None]

        def ada_piece(l, pc):
            AW = ada_piece_bufs[0]
            bi = pc % 2
            S.dma("sp", AW[bi][:], ada_w[l].rearrange("(kc p) c -> p kc c", p=128)[:, :, 512 * pc:512 * pc + 512],
                  writes=[("AW", bi)], key=("aw", bi))
            b = bank()
            for q in range(4):
                mm(b, 2, [(AW[bi][:, kc, 128 * q:128 * q + 128], CCs[:, kc, :]) for kc in range(8)],
                   reads=[("AW", bi), "CCs"], c_off=2 * q)
            S.op("dve", lambda: nc.vector.tensor_copy(
                out=MOD[:, l, 4 * pc:4 * pc + 4, :], in_=PS[b][:, 0:8].rearrange("p (a b) -> p a b", b=2)),
                reads=[("ps", b)], writes=[("MOD", l)])

        def ada_finish(l):
            for s_ in range(2):
                S.op("dve", lambda s_=s_: nc.vector.tensor_tensor(
                    out=MOD[:, l, :, s_], in0=MOD[:, l, :, s_], in1=VEC[:, l * V_L + V_AB:l * V_L + V_AB + 24], op=ALU.add),
                    reads=[("MOD", l), "VEC"], writes=[("MOD", l)])
                S.op("dve", lambda s_=s_: nc.vector.scalar_tensor_tensor(
                    out=GM[:, l, :, s_], in0=MOD[:, l, 8:16, s_], scalar=1.0, in1=VEC[:, l * V_L + V_NG:l * V_L + V_NG + 8],
                    op0=ALU.add, op1=ALU.mult), reads=[("MOD", l), "VEC"], writes=[("GM", l)])

        with ExitStack() as s0:
            CCr = sb(s0, "ccr", [128, 16])
            AW = [sb(s0, "aw%d" % i, [128, 8, 512]) for i in range(2)]
            T1 = sb(s0, "t1", [128, 32])
            T2 = sb(s0, "t2", [128, 32])
            T3 = sb(s0, "t3", [128, 32])
            S.dma("sp", CCr[:], cc, writes=["CCr"], key="c5")
            for s_ in range(2):
                S.op("act", lambda s_=s_: nc.scalar.activation(out=CCs[:, :, s_], in_=CCr[:, 8 * s_:8 * s_ + 8], func=AF.Silu),
                     reads=["CCr"], writes=["CCs"])
            ada_piece_bufs[0] = AW
            for pc in range(6):
                ada_piece(0, pc)
            ada_finish(0)
            for l in range(DEPTH):
                lam = VEC[:, l * V_L + V_LAM:l * V_L + V_LAM + 16]
                e_ = T1[:, 0:16]
                ln_ = T1[:, 16:32]
                ser = T2[:, 0:16]
                msk = T2[:, 16:32]
                tmp = T3[:, 0:16]
                S.op("act", lambda: nc.scalar.activation(out=e_, in_=lam, func=AF.Exp, scale=-1.0), reads=["VEC", "CNEG"], writes=["T1"])
                S.op("act", lambda: nc.scalar.activation(out=ln_, in_=e_, func=AF.Ln, bias=1.0, scale=1.0), reads=["T1"], writes=["T1b"])
                S.op("dve", lambda: nc.vector.tensor_scalar(out=ser, in0=e_, scalar1=-0.25, scalar2=1.0 / 3.0, op0=ALU.mult, op1=ALU.add), reads=["T1"], writes=["T2"])
                S.op("dve", lambda: nc.vector.tensor_tensor(out=ser, in0=ser, in1=e_, op=ALU.mult), reads=["T2", "T1"], writes=["T2"])
                S.op("dve", lambda: nc.vector.tensor_scalar(out=ser, in0=ser, scalar1=-0.5, scalar2=None, op0=ALU.add), reads=["T2"], writes=["T2"])
                S.op("dve", lambda: nc.vector.tensor_tensor(out=ser, in0=ser, in1=e_, op=ALU.mult), reads=["T2", "T1"], writes=["T2"])
                S.op("dve", lambda: nc.vector.tensor_scalar(out=ser, in0=ser, scalar1=1.0, scalar2=None, op0=ALU.add), reads=["T2"], writes=["T2"])
                S.op("dve", lambda: nc.vector.tensor_tensor(out=ser, in0=ser, in1=e_, op=ALU.mult), reads=["T2", "T1"], writes=["T2"])
                S.op("dve", lambda: nc.vector.tensor_scalar(out=msk, in0=e_, scalar1=0.1, scalar2=None, op0=ALU.is_lt), reads=["T1"], writes=["T2m"])
                S.op("dve", lambda: nc.vector.tensor_tensor(out=tmp, in0=ser, in1=ln_, op=ALU.subtract), reads=["T2", "T1b"], writes=["T3"])
                S.op("dve", lambda: nc.vector.tensor_tensor(out=tmp, in0=tmp, in1=msk, op=ALU.mult), reads=["T3", "T2m"], writes=["T3"])
                S.op("dve", lambda: nc.vector.tensor_tensor(out=tmp, in0=tmp, in1=ln_, op=ALU.add), reads=["T3", "T1b"], writes=["T3"])
                S.op("dve", lambda l=l: nc.vector.tensor_scalar(out=CNEG[:, l, :], in0=tmp, scalar1=-8.0, scalar2=None, op0=ALU.mult), reads=["T3"], writes=["CNEG"])
                S.op("dve", lambda l=l: nc.vector.tensor_scalar(out=CNEG2[:, l, :], in0=tmp, scalar1=-4.0, scalar2=None, op0=ALU.mult), reads=["T3"], writes=["CNEG"])
            S.barrier()

        def load_w(dst, src2d, key, reskey):
            S.dma("pool", dst, src2d.rearrange("(kc p) c -> p kc c", p=128), writes=[reskey], key=key)

        def rms_tile(XTb, xkey, n, gm_ap, sh_ap, out_fn, SQ, RS, TMP, okeys, phase="all", rskey="RS"):
            if phase != "b":
                rms_stats(XTb, xkey, n, SQ, RS, rskey)
            if phase != "a":
                rms_apply(XTb, xkey, n, gm_ap, sh_ap, out_fn, RS, TMP, okeys, rskey)

        def rms_stats(XTb, xkey, n, SQ, RS, rskey):
            b = bank()
            for hf in range(2):
                S.op("act", lambda hf=hf: nc.scalar.activation(out=SQ[:, :, 0:n], in_=XTb[:, 4 * hf:4 * hf + 4, 0:n], func=AF.Square),
                     reads=[xkey], writes=["SQ"])
                for q in range(4):
                    S.op("pe", lambda q=q, hf=hf: nc.tensor.matmul(PS[b][:, 0:n], lhsT=ONES[:], rhs=SQ[:, q, 0:n], start=(hf == 0 and q == 0),
                                                                   stop=(hf == 1 and q == 3)),
                         reads=(["ONES", "SQ"] if q == 0 else ()), writes=[("ps", b)], inc=(q == 3))
                S._record((S.engs["pe"]["sem"], S.engs["pe"]["cnt"], "pe"), ["SQ"], [])
            S.op("act", lambda: nc.scalar.activation(out=RS[:, 0:n], in_=PS[b][:, 0:n], func=AF.Sqrt, scale=1.0 / D, bias=EPSC[:, 0:1]),
                 reads=[("ps", b), "EPSC"], writes=[rskey])
            S.op("dve", lambda: nc.vector.reciprocal(out=RS[:, 0:n], in_=RS[:, 0:n]), reads=[rskey], writes=[rskey])

        def rms_apply(XTb, xkey, n, gm_ap, sh_ap, out_fn, RS, TMP, okeys, rskey):
            for fc in range(8):
                tb = fc % 2
                S.op("dve", lambda fc=fc, tb=tb: nc.vector.tensor_tensor(out=TMP[:, tb, 0:n], in0=XTb[:, fc, 0:n], in1=RS[:, 0:n], op=ALU.mult),
                     reads=[xkey, rskey], writes=[("TMP", tb)])
                if sh_ap is None:
                    S.op("act", lambda fc=fc, tb=tb: nc.scalar.activation(out=out_fn(fc), in_=TMP[:, tb, 0:n], func=AF.Identity, scale=gm_ap(fc)),
                         reads=[("TMP", tb), "GM", "VEC"], writes=okeys)
                else:
                    S.op("act", lambda fc=fc, tb=tb: nc.scalar.activation(out=out_fn(fc), in_=TMP[:, tb, 0:n], func=AF.Identity, scale=gm_ap(fc), bias=sh_ap(fc)),
                         reads=[("TMP", tb), "GM", "MOD"], writes=okeys)

        Q16 = sb(st, "q16", [128, 1])
        S.op("dve", lambda: nc.vector.memset(Q16[:], 1.0 / 16), writes=["Q16"])
        EPSC = sb(st, "epsc", [128, 1])
        S.op("dve", lambda: nc.vector.memset(EPSC[:], EPS), writes=["EPSC"])

        for l in range(DEPTH):
            last = (l == DEPTH - 1)
            with ExitStack() as sl:
                HT = sb(sl, "ht", [128, 8, NT], BF16)
                with ExitStack() as s0:
                    XTs = [sb(s0, "xt%d" % i, [128, 8, 512]) for i in range(3)]
                    PTs = [sb(s0, "pt%d" % i, [128, 8, 512]) for i in range(2)]
                    SQ = sb(s0, "sq", [128, 4, 512], BF16)
                    RSs = [sb(s0, "rs%d" % i, [128, 512]) for i in range(2)]
                    TMP = sb(s0, "tmp", [128, 2, 512])
                    if l == 0 and DEPTH > 1:
                        ada_piece_bufs[0] = [sb(s0, "aw2_%d" % i, [128, 8, 512]) for i in range(2)]

                    def p0_a(ti):
                        c0, n = TILES[ti]
                        bi = ti % 3
                        pb = ti % 2
                        XTb = XTs[bi]
                        xkey = ("XT", bi)
                        if l == 0:
                            if ti == 0:
                                S.dma("sp", XTb[:, :, 0:n], ctxT.rearrange("(fc p) t -> p fc t", p=128), writes=[xkey], key=("xt", bi))
                            else:
                                t0 = c0 - L0
                                S.dma("sp", XTb[:, :, 0:n], xT.rearrange("(fc p) t -> p fc t", p=128)[:, :, t0:t0 + n], writes=[xkey], key=("xt", bi))
                                S.dma("sp", PTs[pb][:, :, 0:n], posT.rearrange("(fc p) t -> p fc t", p=128)[:, :, t0:t0 + n], writes=[("PT", pb)], key=("pt", pb))
                                S.op("pool", lambda: nc.gpsimd.tensor_tensor(out=XTb[:], in0=XTb[:], in1=PTs[pb][:], op=ALU.add),
                                     reads=[xkey, ("PT", pb)], writes=[xkey])
                            S.dma("sp", XS[0].rearrange("(fc p) t -> p fc t", p=128)[:, :, c0:c0 + n], XTb[:, :, 0:n], reads=[xkey], writes=[("XS", ti)], key=("xso", bi))
                        else:
                            S.dma("sp", XTb[:, :, 0:n], XS[l].rearrange("(fc p) t -> p fc t", p=128)[:, :, c0:c0 + n], writes=[xkey], key=("xt", bi))
                        rms_stats(XTb, xkey, n, SQ, RSs[pb], ("RS", pb))

                    def p0_b(ti):
                        c0, n = TILES[ti]
                        bi = ti % 3
                        pb = ti % 2
                        s_ = 1 if ti == 0 else 0
                        rms_apply(XTs[bi], ("XT", bi), n, lambda fc: GM[:, l, fc, s_:s_ + 1], lambda fc: MOD[:, l, fc, s_:s_ + 1],
                                  lambda fc: HT[:, fc, c0:c0 + n], RSs[pb], TMP, [("HT", ti)], ("RS", pb))

                    p0_a(0)
                    for ti in range(NTI):
                        if l == 0 and DEPTH > 1 and 1 <= ti <= 6:
                            ada_piece(1, ti - 1)
                            if ti == 6:
                                ada_finish(1)
                        if ti + 1 < NTI:
                            p0_a(ti + 1)
                        p0_b(ti)
                    S.barrier()

                HTK = [("HT", ti) for ti in range(NTI)]

                def slab_proj(WT, wkey, consume):
                    for ti, (c0, n) in enumerate(TILES):
                        b = bank()
                        mm(b, n, [(WT[:, kc, :], HT[:, kc, c0:c0 + n]) for kc in range(8)], reads=[wkey, ("HT", ti)])
                        consume(ti, c0, n, b)

                with ExitStack() as sa:
                    XCb = sb(sa, "xcb", [128, NT], BF16)
                    G = sb(sa, "g", [128, NT], BF16)
                    XC = sb(sa, "xc", [128, NT])
                    WU = [sb(sa, "wu%d" % i, [128, 8, 128], BF16) for i in range(2)]
                    WZ = [sb(sa, "wz%d" % i, [128, 8, 128], BF16) for i in range(2)]
                    PW = sb(sa, "pw", [128, 4, 128], BF16)
                    ZS = [sb(sa, "zs%d" % i, [128, 512]) for i in range(2)]
                    DAT = slice(C0, L0 + SEQ)

                    def tk(name):
                        return [(name, ti) for ti in range(NTI)]

                    S.dma("pool", PW[:], poolw_d[l].rearrange("g i j -> i g j"), writes=["PW"], key="pw")

                    def issue_w(slab_idx):
                        bi = slab_idx % 2
                        if slab_idx < 8:
                            j = slab_idx
                            load_w(WU[bi][:], w_in[l][:, 128 * j:128 * j + 128], ("wu", bi), ("WU", bi))
                            load_w(WZ[bi][:], w_in[l][:, 1024 + 128 * j:1024 + 128 * j + 128], ("wz", bi), ("WZ", bi))
                            S.dma("pool", BD[bi][:], bd_d[l, j], writes=[("BD", bi)], key=("bd", bi))
                        else:
                            g = slab_idx - 8
                            load_w(WU[bi][:], w_in[l][:, 3072 + 128 * g:3072 + 128 * g + 128], ("wu", bi), ("WU", bi))
                            load_w(WZ[bi][:], w_in[l][:, 3584 + 128 * g:3584 + 128 * g + 128], ("wz", bi), ("WZ", bi))

                    def gate_out(slab, y_ap_fn, ykeys, scale_ap=None):
                        bi = slab % 2 if slab < 8 else (slab - 8) % 2
                        if slab >= 12:
                            bi = (slab - 4) % 2

                        def cons(ti, c0, n, b):
                            zb = ti % 2
                            S.op("act", lambda: nc.scalar.activation(out=ZS[zb][:, 0:n], in_=PS[b][:, 0:n], func=AF.Silu),
                                 reads=[("ps", b)], writes=[("ZS", zb)])
                            yap, ykey = y_ap_fn(ti, c0, n)
                            if scale_ap is None:
                                S.op("dve", lambda: nc.vector.tensor_tensor(out=G[:, c0:c0 + n], in0=yap, in1=ZS[zb][:, 0:n], op=ALU.mult),
                                     reads=[("ZS", zb)] + ykey, writes=[("G", ti)])
                            else:
                                S.op("dve", lambda: nc.vector.scalar_tensor_tensor(out=G[:, c0:c0 + n], in0=yap, scalar=scale_ap, in1=ZS[zb][:, 0:n],
                                                                                   op0=ALU.mult, op1=ALU.mult),
                                     reads=[("ZS", zb), "VEC"] + ykey, writes=[("G", ti)])
                        return cons

                    with ExitStack() as sr:
                        U = sb(sr, "u", [128, NT], BF16)
                        HF = sb(sr, "hf", [128, NT])
                        RW = 1536
                        RA = [sb(sr, "ra%d" % i, [128, RW]) for i in range(2)]
                        RSq = [sb(sr, "rs%d" % i, [128, RW]) for i in range(2)]
                        RI = [sb(sr, "ri%d" % i, [128, RW]) for i in range(2)]
                        RH = [sb(sr, "rh%d" % i, [128, RW]) for i in range(2)]
                        TZ = [sb(sr, "tz%d" % i, [128, 512]) for i in range(2)]
                        BD = [sb(sr, "bd%d" % i, [128, 4, 128], BF16) for i in range(2)]
                        DGs = [sb(sr, "dg%d" % i, [128, 4, 128], BF16) for i in range(2)]
                        HB5 = sb(sr, "hb5", [128, 32])
                        S.op("pool", lambda: nc.gpsimd.memset(U[:], 0.0), writes=tk("U") + [("U", "pad")])
                        S.op("dve", lambda: nc.vector.tensor_scalar(out=HB5[:], in0=VEC[:, l * V_L + V_BA:l * V_L + V_BA + 32], scalar1=0.5, scalar2=None, op0=ALU.mult),
                             reads=["VEC"], writes=["HB5"])
                        FWD = [(C0, 1312, [0, 1, 2]), (1312, 2848, [3, 4, 5]), (2848, 4384, [6, 7, 8])]
                        BWD = [(C0, C0 + CTX, [0]), (2848, 4384, [8, 7, 6]), (1312, 2848, [5, 4, 3]), (L0, 1312, [2, 1])]
                        gcount = [0]
                        def uproj_tile(jn, ti):
                            c0, n = TILES[ti]
                            b = bank()
                            mm(b, n, [(WU[jn % 2][:, kc, :], HT[:, kc, c0:c0 + n]) for kc in range(8)], reads=[("WU", jn % 2), ("HT", ti)])
                            S.op("dve", lambda: nc.vector.tensor_copy(out=U[:, c0:c0 + n], in_=PS[b][:, 0:n]), reads=[("ps", b)], writes=[("U", ti)])

                        issue_w(0)
                        for ti in range(NTI):
                            uproj_tile(0, ti)
                        for j in range(8):
                            bi = j % 2
                            issue_w(j + 1)
                            upend = list(range(NTI)) if j + 1 < 8 else []
                            DG = DGs[bi]
                            for tap in range(4):
                                S.op("dve", lambda tap=tap, j=j: nc.vector.tensor_scalar(
                                    out=DG[:, tap, :], in0=IDENT[:], scalar1=vcol(l, V_CW, tap * 8 + j), scalar2=None, op0=ALU.mult),
                                    reads=["IDENT", "VEC"], writes=[("DG", bi)])

                            for ti, (c0, n) in enumerate(TILES):
                                b = bank()
                                mm(b, n, [(DG[:, tap, :], U[:, c0 + tap - 2:c0 + tap - 2 + n]) for tap in range(4)],
                                   reads=[("DG", bi), ("U", "pad")] + tk("U"))
                                S.op("dve", lambda c0=c0, n=n, b=b: nc.vector.tensor_scalar(out=XC[:, c0:c0 + n], in0=PS[b][:, 0:n], scalar1=vcol(l, V_CB, j),
                                                                                           scalar2=None, op0=ALU.add),
                                     reads=[("ps", b), "VEC"], writes=[("XC", ti)])
                                S.op("dve", lambda c0=c0, n=n, b=b: nc.vector.tensor_scalar(out=XCb[:, c0:c0 + n], in0=PS[b][:, 0:n], scalar1=vcol(l, V_CB, j),
                                                                                           scalar2=None, op0=ALU.add),
                                     reads=[("ps", b), "VEC"], writes=[("XCb", ti)])
                            for d in range(2):
                                groups = FWD if d == 0 else BWD
                                cidx = d * 8 + j
                                c_full = CNEG[:, l, cidx:cidx + 1]
                                c_half = CNEG2[:, l, cidx:cidx + 1]
                                pend = None

                                def stage2(gi, g0, g1, tl, rs):
                                    w = g1 - g0
                                    rk = [("RA", rs), ("RI", rs)]
                                    if d == 0:
                                        if gi == 0:
                                            S.op("dve", lambda: nc.vector.tensor_tensor_scan(out=HF[:, C0:C0 + CTX], data0=RA[rs][:, 0:CTX], data1=RI[rs][:, 0:CTX],
                                                                                             initial=0.0, op0=ALU.mult, op1=ALU.add),
                                                 reads=rk, writes=[("HF", 0)])
                                            o = L0 - g0
                                            S.op("dve", lambda: nc.vector.tensor_tensor_scan(out=HF[:, L0:g1], data0=RA[rs][:, o:w], data1=RI[rs][:, o:w],
                                                                                             initial=HF[:, C0 + CTX - 1:C0 + CTX], op0=ALU.mult, op1=ALU.add),
                                                 reads=rk + [("HF", 0)], writes=[("HF", t) for t in tl[1:]])
                                        else:
                                            S.op("dve", lambda: nc.vector.tensor_tensor_scan(out=HF[:, g0:g1], data0=RA[rs][:, 0:w], data1=RI[rs][:, 0:w],
                                                                                             initial=HF[:, g0 - 1:g0], op0=ALU.mult, op1=ALU.add),
                                                 reads=rk + tk("HF"), writes=[("HF", t) for t in tl])
                                        return
                                    init = 0.0 if gi == 0 else RH[1 - rs][:, 0:1]
                                    S.op("dve", lambda: nc.vector.tensor_tensor_scan(out=RH[rs][:, w - 1::-1], data0=RA[rs][:, w - 1::-1], data1=RI[rs][:, w - 1::-1],
                                                                                     initial=init, op0=ALU.mult, op1=ALU.add),
                                         reads=rk + [("RH", 1 - rs)], writes=[("RH", rs)])
                                    S.op("pool", lambda: nc.gpsimd.tensor_tensor(out=RI[rs][:, 0:w], in0=RH[rs][:, 0:w], in1=HF[:, g0:g1], op=ALU.add),
                                         reads=[("RH", rs)] + [("HF", t) for t in tl], writes=[("RI", rs)])
                                    for t in tl:
                                        c0, n = TILES[t]
                                        zb = t % 2
                                        b = bank()
                                        mm(b, n, [(WZ[bi][:, kc, :], HT[:, kc, c0:c0 + n]) for kc in range(8)], reads=[("WZ", bi), ("HT", t)])
                                        S.op("act", lambda b=b, n=n, zb=zb: nc.scalar.activation(out=TZ[zb][:, 0:n], in_=PS[b][:, 0:n], func=AF.Tanh, scale=0.5),
                                             reads=[("ps", b)], writes=[("TZ", zb)])
                                        S.op("dve", lambda b=b, n=n, zb=zb: nc.vector.scalar_tensor_tensor(out=TZ[zb][:, 0:n], in0=TZ[zb][:, 0:n], scalar=1.0, in1=PS[b][:, 0:n],
                                                                                                        op0=ALU.add, op1=ALU.mult),
                                             reads=[("ps", b), ("TZ", zb)], writes=[("TZ", zb)])
                                        S.op("pool", lambda c0=c0, n=n, zb=zb: nc.gpsimd.tensor_tensor(out=G[:, c0:c0 + n], in0=RI[rs][:, c0 - g0:c0 - g0 + n], in1=TZ[zb][:, 0:n], op=ALU.mult),
                                             reads=[("RI", rs), ("TZ", zb)], writes=[("G", t)])

                                for gi, (g0, g1, tl) in enumerate(groups):
                                    rs = gcount[0] % 2
                                    gcount[0] += 1
                                    w = g1 - g0
                                    for t in tl:
                                        c0, n = TILES[t]
                                        o = c0 - g0
                                        b = bank()
                                        mm(b, n, [(BD[bi][:, 2 * d, :], XCb[:, c0:c0 + n])], reads=[("BD", bi), ("XCb", t)])
                                        S.op("act", lambda o=o, n=n, b=b: nc.scalar.activation(out=RA[rs][:, o:o + n], in_=PS[b][:, 0:n], func=AF.Tanh,
                                                                                               bias=HB5[:, cidx:cidx + 1], scale=0.5),
                                             reads=[("ps", b), "HB5"], writes=[("RA", rs)])
                                        b2 = bank()
                                        mm(b2, n, [(BD[bi][:, 2 * d + 1, :], XCb[:, c0:c0 + n])], reads=[("BD", bi), ("XCb", t)])
                                        S.op("act", lambda o=o, n=n, b2=b2: nc.scalar.activation(out=RI[rs][:, o:o + n], in_=PS[b2][:, 0:n], func=AF.Tanh,
                                                                                                 bias=HB5[:, 16 + cidx:16 + cidx + 1], scale=0.5),
                                             reads=[("ps", b2), "HB5"], writes=[("RI", rs)])
                                    S.op("act", lambda: nc.scalar.activation(out=RSq[rs][:, 0:w], in_=RA[rs][:, 0:w], func=AF.Exp, scale=c_full, bias=c_full),
                                         reads=[("RA", rs), "CNEG"], writes=[("RS", rs)])
                                    S.op("act", lambda: nc.scalar.activation(out=RA[rs][:, 0:w], in_=RA[rs][:, 0:w], func=AF.Exp, scale=c_half, bias=c_half),
                                         reads=[("RA", rs), "CNEG"], writes=[("RA", rs)])
                                    S.op("act", lambda: nc.scalar.activation(out=RSq[rs][:, 0:w], in_=RSq[rs][:, 0:w], func=AF.Sqrt, scale=-1.0 / 16, bias=Q16[:, 0:1]),
                                         reads=[("RS", rs), "Q16"], writes=[("RS", rs)])
                                    S.op("dve", lambda: nc.vector.scalar_tensor_tensor(out=RI[rs][:, 0:w], in0=RI[rs][:, 0:w], scalar=1.0, in1=XC[:, g0:g1],
                                                                                       op0=ALU.add, op1=ALU.mult),
                                         reads=[("RI", rs)] + [("XC", t) for t in tl], writes=[("RI", rs)])
                                    S.op("pool", lambda: nc.gpsimd.tensor_tensor(out=RI[rs][:, 0:w], in0=RI[rs][:, 0:w], in1=RSq[rs][:, 0:w], op=ALU.mult),
                                         reads=[("RI", rs), ("RS", rs)], writes=[("RI", rs)])
                                    if pend is not None:
                                        stage2(*pend)
                                    pend = (gi, g0, g1, tl, rs)
                                    for _ in range(2):
                                        if upend:
                                            uproj_tile(j + 1, upend.pop(0))
                                stage2(*pend)
                            while upend:
                                uproj_tile(j + 1, upend.pop(0))
                            S.dma("sp", GT[l][128 * j:128 * j + 128, :], G[:], reads=tk("G"), writes=[("GT", j)], key="gst")
                        S.barrier()

                    A = sb(sa, "a", [128, NT])
                    Sb = sb(sa, "s", [128, NT])
                    IB = sb(sa, "ib", [128, NT])
                    S.op("pool", lambda: nc.gpsimd.memset(XC[:], 0.0), reads=tk("XC"), writes=tk("XC") + [("XC", "pad")])
                    for g in range(4):
                        sidx = 8 + g
                        bi = sidx % 2
                        if g < 3:
                            issue_w(sidx + 1)
                        win = 2 << g

                        def cons_up(ti, c0, n, b):
                            S.op("act", lambda: nc.scalar.activation(out=XC[:, c0:c0 + n], in_=PS[b][:, 0:n], func=AF.Identity),
                                 reads=[("ps", b)], writes=[("XC", ti)])
                        slab_proj(WU[bi], ("WU", bi), cons_up)
                        allx = tk("XC") + [("XC", "pad")]
                        S.op("dve", lambda: nc.vector.tensor_tensor(out=A[:, 9:NT - 9], in0=XC[:, 8:NT - 10], in1=XC[:, 9:NT - 9], op=ALU.add),
                             reads=allx + tk("A"), writes=tk("A"))
                        cur, curk = A, "A"
                        if g >= 1:
                            S.op("dve", lambda: nc.vector.tensor_tensor(out=Sb[:, 10:NT - 10], in0=A[:, 9:NT - 11], in1=A[:, 11:NT - 9], op=ALU.add),
                                 reads=tk("A") + tk("S"), writes=tk("S"))
                            cur, curk = Sb, "S"
                        if g >= 2:
                            S.op("dve", lambda: nc.vector.tensor_tensor(out=A[:, 12:NT - 12], in0=Sb[:, 10:NT - 14], in1=Sb[:, 14:NT - 10], op=ALU.add),
                                 reads=tk("S") + tk("A"), writes=tk("A"))
                            cur, curk = A, "A"
                        if g >= 3:
                            S.op("dve", lambda: nc.vector.tensor_tensor(out=Sb[:, 16:NT - 16], in0=A[:, 12:NT - 20], in1=A[:, 20:NT - 12], op=ALU.add),
                                 reads=tk("A") + tk("S"), writes=tk("S"))
                            cur, curk = Sb, "S"
                        S.op("dve", lambda cur=cur: nc.vector.scalar_tensor_tensor(out=XCb[:, DAT], in0=cur[:, DAT], scalar=1.0 / win, in1=XC[:, DAT],
                                                                                   op0=ALU.mult, op1=ALU.subtract),
                             reads=tk(curk) + allx + tk("XCb"), writes=tk("XCb"))
                        hw = win // 2
                        for (s0_, Ls) in ((C0, CTX), (L0, SEQ)):
                            for side in range(2):
                                ncol = hw if side == 0 else hw - 1
                                if ncol == 0:
                                    continue
                                cst = s0_ if side == 0 else s0_ + Ls - ncol
                                tb = EDGE[:, g * 16 + side * 8:g * 16 + side * 8 + ncol]
                                S.op("dve", lambda cur=cur, cst=cst, ncol=ncol, tb=tb: nc.vector.tensor_tensor(
                                    out=IB[:, cst:cst + ncol], in0=cur[:, cst:cst + ncol], in1=tb, op=ALU.mult),
                                    reads=tk(curk) + ["EDGE"] + tk("IB"), writes=tk("IB"))
                                S.op("dve", lambda cst=cst, ncol=ncol: nc.vector.tensor_tensor(
                                    out=XCb[:, cst:cst + ncol], in0=IB[:, cst:cst + ncol], in1=XC[:, cst:cst + ncol], op=ALU.subtract),
                                    reads=tk("IB") + allx + tk("XCb"), writes=tk("XCb"))
                        ybanks = {}

                        def y_fn(ti, c0, n):
                            b = bank()
                            mm(b, n, [(PW[:, g, :], XCb[:, c0:c0 + n])], reads=["PW", ("XCb", ti)])
                            return PS[b][:, 0:n], [("ps", b)]
                        slab_proj(WZ[bi], ("WZ", bi), gate_out(12 + g, y_fn, None, scale_ap=vcol(l, V_PS, g)))
                        S.dma("sp", GT[l][128 * (12 + g):128 * (12 + g) + 128, :], G[:], reads=tk("G"), writes=[("GT", 12 + g)], key="gst")
                    S.barrier()

                with ExitStack() as sf:
                    PQc = sb(sf, "pqc", [128, 2, 4, 256], BF16)
                    ZPQ = sb(sf, "zpq", [128, 4, 8, 4, 256], BF16)
                    WZF = sb(sf, "wzf", [128, 4, 8, 128], BF16)
                    FW = sb(sf, "fw", [128, 4, 128], BF16)
                    CS4 = sb(sf, "cs4", [128, 4, 4, 256], BF16)
                    ZS = [sb(sf, "zsf%d" % i, [128, 512]) for i in range(2)]
                    GS = [sb(sf, "gs%d" % i, [128, 512], BF16) for i in range(2)]
                    S.dma("sp", CS4[:], cs4_d, writes=["CS4"], key="cs4")
                    S.dma("pool", FW[:], fftw_d[l].rearrange("g i j -> i g j"), writes=["FW"], key="fw")
                    for g in range(4):
                        load_w(WZF[:, g], w_in[l][:, 2560 + 128 * g:2560 + 128 * g + 128], ("wzf", g), ("WZF", g))
                    with ExitStack() as sf1:
                        UF = sb(sf1, "uf", [128, NT], BF16)
                        WUF = [sb(sf1, "wuf%d" % i, [128, 8, 128], BF16) for i in range(2)]
                        load_w(WUF[0][:], w_in[l][:, 2048:2048 + 128], ("wuf", 0), ("WUF", 0))
                        for g in range(4):
                            bi = g % 2
                            if g < 3:
                                load_w(WUF[1 - bi][:], w_in[l][:, 2048 + 128 * (g + 1):2048 + 128 * (g + 2)], ("wuf", 1 - bi), ("WUF", 1 - bi))
                            if g == 2:
                                S.dma("pool", WBS[l][0:1024, :], proj_a[l], writes=["WBS"], key="wbs")
                                S.dma("pool", WBS[l][1024:1536, :], proj_b[l], writes=["WBS"], key="wbs")
                                S.dma("pool", WBS[l][1536:2048, :], proj_c[l], writes=["WBS"], key="wbs")
                                S.dma("pool", WBS[l][2048:3072, :], w_out[l], writes=["WBS"], key="wbs")
                                for h in range(2):
                                    S.dma("pool", WGS[l][:, 1536 * h:1536 * h + 1536], w_in[l][:, 4096 + 1536 * h:4096 + 1536 * h + 1536], writes=["WGS"], key="wgs")

                            def cons_uf(ti, c0, n, b):
                                if ti % 2 == 0:
                                    S.op("dve", lambda: nc.vector.tensor_copy(out=UF[:, c0:c0 + n], in_=PS[b][:, 0:n]), reads=[("ps", b)], writes=[("UF", ti)])
                                else:
                                    S.op("act", lambda: nc.scalar.activation(out=UF[:, c0:c0 + n], in_=PS[b][:, 0:n], func=AF.Identity),
                                         reads=[("ps", b)], writes=[("UF", ti)])
                            slab_proj(WUF[bi], ("WUF", bi), cons_uf)
                            ufk = [("UF", ti) for ti in range(NTI)]
                            b = bank()
                            for tc in range(2):
                                mm(b, 256, [(UF[:, C0 + 128 * tc:C0 + 128 * tc + 128], CS128[:])], reads=ufk + ["CS128"], c_off=256 * tc)
                            S.op("dve", lambda b=b, g=g: nc.vector.tensor_copy(out=PQc[:, :, g, :], in_=PS[b][:].rearrange("p (a c) -> p a c", c=256)),
                                 reads=[("ps", b)], writes=["PQc"])
                            ev = 0
                            for m in range(4):
                                for cp in range(4):
                                    b = bank()
                                    for h in range(2):
                                        ch = 2 * cp + h
                                        mm(b, 256, [(UF[:, L0 + 1024 * q + 128 * ch:L0 + 1024 * q + 128 * ch + 128], CS4[:, m, q, :]) for q in range(4)],
                                           reads=ufk + ["CS4"], c_off=256 * h)
                                    dst = ZPQ[:, m, 2 * cp:2 * cp + 2, g, :]
                                    src = PS[b][:].rearrange("p (a c) -> p a c", c=256)
                                    if ev % 2 == 0:
                                        S.op("dve", lambda dst=dst, src=src: nc.vector.tensor_copy(out=dst, in_=src), reads=[("ps", b)], writes=["ZPQ"])
                                    else:
                                        S.op("act", lambda dst=dst, src=src: nc.scalar.activation(out=dst, in_=src, func=AF.Identity), reads=[("ps", b)], writes=["ZPQ"])
                                    ev += 1
                        S.barrier()

                    with ExitStack() as sf2:
                        FBall = sb(sf2, "fball", [128, 4, 2048], BF16)
                        FBc = sb(sf2, "fbc", [128, 4, 256], BF16)
                        CB = [sb(sf2, "cb%d" % i, [128, 4, 512], BF16) for i in range(2)]
                        SBf = [sb(sf2, "sbf%d" % i, [128, 4, 512], BF16) for i in range(2)]
                        b4 = [0]

                        def bank4():
                            b = 4 + b4[0] % 4
                            b4[0] += 1
                            return b

                        def epilogue(g, c0, n, fb_ap, fbkey, ti):
                            e = (g + ti) % 2
                            by = bank4()
                            mm(by, n, [(FW[:, g, :], fb_ap)], reads=["FW"] + fbkey)
                            bz = bank4()
                            mm(bz, n, [(WZF[:, g, kc, :], HT[:, kc, c0:c0 + n]) for kc in range(8)], reads=[("WZF", g), ("HT", ti)])
                            S.op("act", lambda: nc.scalar.activation(out=ZS[e][:, 0:n], in_=PS[bz][:, 0:n], func=AF.Silu), reads=[("ps", bz)], writes=[("ZSF", e)])
                            S.op("dve", lambda: nc.vector.tensor_tensor(out=GS[e][:, 0:n], in0=PS[by][:, 0:n], in1=ZS[e][:, 0:n], op=ALU.mult),
                                 reads=[("ps", by), ("ZSF", e)], writes=[("GS", e)])
                            S.dma("sp", GT[l][128 * (8 + g):128 * (8 + g) + 128, c0:c0 + n], GS[e][:, 0:n], reads=[("GS", e)], writes=[("GT", 8 + g, ti)], key=("gs", e))

                        for g in range(4):
                            pairs = []
                            for tc in range(2):
                                pairs.append((PQc[:, tc, g, 0:128], CS256[:, 0, tc, :]))
                                pairs.append((PQc[:, tc, g, 128:256], CS256[:, 1, tc, :]))
                            mm(g, 256, pairs, reads=["PQc", "CS256"])
                            S.op("act", lambda g=g: nc.scalar.activation(out=FBc[:, g, :], in_=PS[g][:, 0:256], func=AF.Identity),
                                 reads=[("ps", g)], writes=[("FBc", g)])
                        for g in range(4):
                            epilogue(g, C0, 256, FBc[:, g, :], [("FBc", g)], 0)
                        ring = [0]
                        blocks = [(kp, m, hf) for kp in range(2) for m in range(4) for hf in range(2)]

                        def issue_dft(i):
                            kp, m, hf = blocks[i]
                            r = i % 2
                            S.dma("sp", CB[r][:], dft_d[m, kp, 0][:, 4 * hf:4 * hf + 4, :], writes=[("CB", r)], key=("cb", r))
                            S.dma("sp", SBf[r][:], dft_d[m, kp, 1][:, 4 * hf:4 * hf + 4, :], writes=[("SBf", r)], key=("sbf", r))
                        issue_dft(0)
                        for i, (kp, m, hf) in enumerate(blocks):
                            r = i % 2
                            if i + 1 < len(blocks):
                                issue_dft(i + 1)
                            for cl in range(4):
                                ch = 4 * hf + cl
                                first = (hf == 0 and cl == 0)
                                lastm = (hf == 1 and cl == 3)
                                for g in range(4):
                                    S.op("pe", lambda g=g, ch=ch, cl=cl, first=first: nc.tensor.matmul(
                                        PS[g][:, 0:512], lhsT=ZPQ[:, m, ch, g, 0:128], rhs=CB[r][:, cl, :], start=first, stop=False),
                                        reads=(["ZPQ", ("CB", r), ("SBf", r)] if cl == 0 and g == 0 else ()), writes=[("ps", g)], inc=False)
                                    S.op("pe", lambda g=g, ch=ch, cl=cl, lastm=lastm: nc.tensor.matmul(
                                        PS[g][:, 0:512], lhsT=ZPQ[:, m, ch, g, 128:256], rhs=SBf[r][:, cl, :], start=False, stop=lastm),
                                        reads=(), writes=[("ps", g)], inc=(cl == 3 and g == 3))
                            S._record((S.engs["pe"]["sem"], S.engs["pe"]["cnt"], "pe"), [("CB", r), ("SBf", r)], [])
                            if hf == 1:
                                for g in range(4):
                                    dst = FBall[:, g, :].rearrange("p (k f) -> p k f", f=4)[:, :, m]
                                    if g % 2 == 0:
                                        S.op("act", lambda g=g, dst=dst: nc.scalar.activation(out=dst, in_=PS[g][:, 0:512], func=AF.Identity),
                                             reads=[("ps", g)], writes=[("FBall", g)])
                                    else:
                                        S.op("dve", lambda g=g, dst=dst: nc.vector.tensor_copy(out=dst, in_=PS[g][:, 0:512]),
                                             reads=[("ps", g)], writes=[("FBall", g)])
                                if m == 3:
                                    for it in range(4):
                                        ti = 1 + 4 * kp + it
                                        for g in range(4):
                                            epilogue(g, L0 + 2048 * kp + 512 * it, 512, FBall[:, g, 512 * it:512 * it + 512], [("FBall", g)], ti)
                        S.barrier()
            S.barrier()

            with ExitStack() as sbk:
                PA = sb(sbk, "pa", [128, 8, 1024], BF16)
                PBt = sb(sbk, "pb", [128, 4, 1024], BF16)
                PCt = sb(sbk, "pc", [128, 4, 1024], BF16)
                WO = sb(sbk, "wo", [128, 8, 1024], BF16)
                WG = sb(sbk, "wg", [128, 8, 3072], BF16)
                XTs = [sb(sbk, "bxt%d" % i, [128, 8, 512]) for i in range(2)]
                GTts = [sb(sbk, "gtt%d" % i, [128, 16, 512], BF16) for i in range(2)]
                HTts = [sb(sbk, "htt%d" % i, [128, 8, 512], BF16) for i in range(2)]
                SQ = sb(sbk, "bsq", [128, 4, 512], BF16)
                MT = sb(sbk, "mt", [128, 8, 512], BF16)
                RS = sb(sbk, "brs", [128, 512])
                TMP = sb(sbk, "btmp", [128, 2, 512])
                SG = [sb(sbk, "sg%d" % i, [128, 512]) for i in range(2)]
                MM = [sb(sbk, "mm%d" % i, [128, 512]) for i in range(2)]
                S.dma("sp", PA[:], WBS[l][0:1024, :].rearrange("(kc p) c -> p kc c", p=128), writes=["PA"], key="wpa")
                S.dma("sp", PBt[:], WBS[l][1024:1536, :].rearrange("(kc p) c -> p kc c", p=128), writes=["PBt"], key="wpb")
                S.dma("sp", PCt[:], WBS[l][1536:2048, :].rearrange("(kc p) c -> p kc c", p=128), writes=["PCt"], key="wpc")
                for h in range(2):
                    S.dma("sp", WG[:, 4 * h:4 * h + 4, :], WGS[l][512 * h:512 * h + 512, :].rearrange("(kc p) c -> p kc c", p=128), writes=["WG"], key="wwg")
                S.dma("sp", WO[:], WBS[l][2048:3072, :].rearrange("(kc p) c -> p kc c", p=128), writes=["WO"], key="wwo")
                tiles = list(enumerate(TILES))
                if last:
                    tiles = tiles[1:]

                def prefetch_x(idx):
                    ti, (c0, n) = tiles[idx]
                    bi = idx % 2
                    S.dma("sp", XTs[bi][:, :, 0:n], XS[l].rearrange("(fc p) t -> p fc t", p=128)[:, :, c0:c0 + n],
                          writes=[("BXT", bi)], key=("bxt", bi))

                def prefetch_g(idx):
                    ti, (c0, n) = tiles[idx]
                    bi = idx % 2
                    S.dma("sp", GTts[bi][:, :, 0:n], GT[l].rearrange("(s p) t -> p s t", p=128)[:, :, c0:c0 + n], writes=[("GTt", bi)], key=("gtt", bi))

                def hstage(idx):
                    ti, (c0, n) = tiles[idx]
                    bi = idx % 2
                    s_ = 1 if ti == 0 else 0
                    rms_tile(XTs[bi], ("BXT", bi), n, lambda fc: GM[:, l, fc, s_:s_ + 1], lambda fc: MOD[:, l, fc, s_:s_ + 1],
                             lambda fc: HTts[bi][:, fc, 0:n], SQ, RS, TMP, [("HTt", bi)])

                prefetch_x(0)
                prefetch_g(0)
                hstage(0)
                if len(tiles) > 1:
                    prefetch_x(1)
                    prefetch_g(1)
                for idx, (ti, (c0, n)) in enumerate(tiles):
                    bi = idx % 2
                    XTb = XTs[bi]
                    GTt = GTts[bi]
                    HTt = HTts[bi]
                    xkey = ("BXT", bi)
                    s_ = 1 if ti == 0 else 0
                    for oc in range(8):
                        if oc == 4 and idx + 1 < len(tiles):
                            hstage(idx + 1)
                        osl = slice(128 * oc, 128 * oc + 128)
                        yb_ = []
                        for br, (Wt, wk, nk, s0_) in enumerate(((PA, "PA", 8, 0), (PBt, "PBt", 4, 8), (PCt, "PCt", 4, 12))):
                            b = bank()
                            mm(b, n, [(Wt[:, kc, osl], GTt[:, s0_ + kc, 0:n]) for kc in range(nk)], reads=[wk, ("GTt", bi)])
                            yb_.append(b)
                        for br in range(3):
                            b = bank()
                            sgi = br % 2
                            mm(b, n, [(WG[:, kc, 1024 * br + 128 * oc:1024 * br + 128 * oc + 128], HTt[:, kc, 0:n]) for kc in range(8)], reads=["WG", ("HTt", bi)])
                            S.op("act", lambda b=b, sgi=sgi: nc.scalar.activation(out=SG[sgi][:, 0:n], in_=PS[b][:, 0:n], func=AF.Sigmoid),
                                 reads=[("ps", b)], writes=[("SG", sgi)])
                            if br == 0:
                                S.op("dve", lambda: nc.vector.tensor_tensor(out=MM[0][:, 0:n], in0=PS[yb_[0]][:, 0:n], in1=SG[0][:, 0:n], op=ALU.mult),
                                     reads=[("ps", yb_[0]), ("SG", 0)], writes=[("MM", 0)])
                            elif br == 1:
                                S.op("dve", lambda: nc.vector.tensor_tensor(out=MM[1][:, 0:n], in0=PS[yb_[1]][:, 0:n], in1=SG[1][:, 0:n], op=ALU.mult),
                                     reads=[("ps", yb_[1]), ("SG", 1)], writes=[("MM", 1)])
                                S.op("pool", lambda: nc.gpsimd.tensor_tensor(out=MM[0][:, 0:n], in0=MM[0][:, 0:n], in1=MM[1][:, 0:n], op=ALU.add),
                                     reads=[("MM", 0), ("MM", 1)], writes=[("MM", 0)])
                            else:
                                S.op("dve", lambda: nc.vector.tensor_tensor(out=MM[1][:, 0:n], in0=PS[yb_[2]][:, 0:n], in1=SG[0][:, 0:n], op=ALU.mult),
                                     reads=[("ps", yb_[2]), ("SG", 0)], writes=[("MM", 1)])
                                S.op("pool", lambda oc=oc: nc.gpsimd.tensor_tensor(out=MT[:, oc, 0:n], in0=MM[0][:, 0:n], in1=MM[1][:, 0:n], op=ALU.add),
                                     reads=[("MM", 0), ("MM", 1)], writes=[("MT", oc)])
                    for o in range(8):
                        b = bank()
                        mm(b, n, [(WO[:, oc, 128 * o:128 * o + 128], MT[:, oc, 0:n]) for oc in range(8)], reads=["WO"] + [("MT", oc) for oc in range(8)])
                        S.op("dve", lambda o=o, b=b: nc.vector.scalar_tensor_tensor(out=XTb[:, o, 0:n], in0=PS[b][:, 0:n], scalar=MOD[:, l, 16 + o, s_:s_ + 1],
                                                                                    in1=XTb[:, o, 0:n], op0=ALU.mult, op1=ALU.add),
                             reads=[("ps", b), "MOD", xkey], writes=[xkey])
                    if not last:
                        S.dma("sp", XS[l + 1].rearrange("(fc p) t -> p fc t", p=128)[:, :, c0:c0 + n], XTb[:, :, 0:n],
                              reads=[xkey], writes=[("XS", ti)], key=("bxo", bi))
                    else:
                        rms_tile(XTb, xkey, n, lambda fc: VEC[:, V_FG + fc:V_FG + fc + 1], None,
                                 lambda fc: XTb[:, fc, 0:n], SQ, RS, TMP, [xkey])
                        t0 = c0 - L0
                        S.dma("sp", outT.rearrange("(fc p) t -> p fc t", p=128)[:, :, t0:t0 + n], XTb[:, :, 0:n],
                              reads=[xkey], writes=[("OT", ti)], key=("ot", bi))
                    if idx + 2 < len(tiles):
                        prefetch_x(idx + 2)
                        prefetch_g(idx + 2)
                S.barrier()
        for k, Dm in S.dsems.items():
            nc.sync.wait_ge(Dm["sem"], Dm["cnt"])
    return nc


def _consts():
    c = {}
    rows = SEQ // 64
    gr, gc = np.meshgrid(np.arange(rows), np.arange(64), indexing="ij")
    quarter = D // 4
    omega = (1.0 / (np.float32(10000.0) ** (np.arange(quarter, dtype=np.float32) / np.float32(quarter)))).astype(np.float32)

    def emb(p):
        ang = p.reshape(-1).astype(np.float32)[:, None] * omega[None, :]
        return np.concatenate([np.sin(ang), np.cos(ang)], axis=-1)
    pos = np.concatenate([emb(gr), emb(gc)], axis=-1).astype(np.float32)
    c["posT"] = np.ascontiguousarray(pos.T)
    c["ident"] = np.eye(128, dtype=np.float32)
    edge = np.ones((4, 2, 8), np.float32)
    for g in range(4):
        hw = (2 << g) // 2
        for m in range(hw):
            edge[g, 0, m] = 1.0 / (m + hw)
        for m in range(hw - 1):
            edge[g, 1, m] = 1.0 / (2 * hw - 1 - m)
    c["edge"] = np.ascontiguousarray(np.broadcast_to(edge.reshape(1, 64), (128, 64))).astype(np.float32)
    i = np.arange(128)
    ang = 2.0 * np.pi * ((i[:, None] * i[None, :]) % 128) / 128.0
    c["cs128"] = (np.concatenate([np.cos(ang), np.sin(ang)], axis=1) / np.sqrt(128.0)).astype(ml_dtypes.bfloat16)
    t = np.arange(256)
    a2 = 2.0 * np.pi * ((t[:, None] * t[None, :]) % 256) / 256.0
    cs = np.stack([np.cos(a2), -np.sin(a2)], axis=0) / 16.0
    c["cs256"] = np.ascontiguousarray(cs.reshape(2, 2, 128, 256).transpose(2, 0, 1, 3)).astype(ml_dtypes.bfloat16)
    tab = 2.0 * np.pi * np.arange(4096) / 4096.0
    ct = (np.cos(tab) / 64.0).astype(np.float32)
    sn = (-np.sin(tab) / 64.0).astype(np.float32)
    tp = np.arange(1024, dtype=np.int64)
    dft = np.empty((4, 2, 2, 128, 8, 512), dtype=ml_dtypes.bfloat16)
    for m in range(4):
        kk = 4 * np.arange(1024, dtype=np.int64) + m
        idx = (tp[:, None] * kk[None, :]) % 4096
        for cs_i, tb in enumerate((ct, sn)):
            mt = tb[idx].astype(ml_dtypes.bfloat16).reshape(8, 128, 2, 512)
            dft[m, :, cs_i] = mt.transpose(2, 1, 0, 3)
    c["dft"] = dft
    cm = np.cos(ang) / np.sqrt(128.0)
    sm = np.sin(ang) / np.sqrt(128.0)
    cs4 = np.empty((128, 4, 4, 256), np.float32)
    for m in range(4):
        for q in range(4):
            cq = float(np.round(np.cos(2.0 * np.pi * m * q / 4.0)))
            sq = float(np.round(np.sin(2.0 * np.pi * m * q / 4.0)))
            cs4[:, m, q, 0:128] = cq * cm - sq * sm
            cs4[:, m, q, 128:256] = cq * sm + sq * cm
    c["cs4"] = cs4.astype(ml_dtypes.bfloat16)
    return c


def _vecs(inp):
    v = np.zeros((128, NV), np.float32)

    def put(col, arr):
        a = np.asarray(arr, np.float32).reshape(-1, 128)
        v[:, col:col + a.shape[0]] = a.T
    for l in range(DEPTH):
        o = l * V_L
        put(o + V_NG, inp["norm_g"][l])
        put(o + V_AB, inp["ada_b"][l])
        put(o + V_CW, inp["conv_w"][l])
        put(o + V_CB, inp["conv_b"][l])
        put(o + V_BA, inp["lru_ba"][l])
        put(o + V_BX, inp["lru_bx"][l])
        put(o + V_LAM, inp["lru_lam"][l])
        put(o + V_PS, inp["pool_scale"][l])
    put(V_FG, inp["final_g"])
    return v


def _blockdiag(inp):
    bd = np.zeros((DEPTH, 8, 128, 4, 128), np.float32)
    for l in range(DEPTH):
        for d in range(2):
            for gi, nm in enumerate(("lru_wa", "lru_wx")):
                w = np.asarray(inp[nm][l, d], np.float32)
                for j in range(8):
                    for hh in range(2):
                        bd[l, j, 64 * hh:64 * hh + 64, 2 * d + gi, 64 * hh:64 * hh + 64] = w[2 * j + hh]
    return bd


def _in_maps(inp, cores):
    c = _consts()
    f = lambda k: np.ascontiguousarray(np.asarray(inp[k], np.float32))
    shared = dict(c)
    shared["vecs"] = _vecs(inp)
    shared["bd"] = _blockdiag(inp)
    for k in ("ada_w", "w_in", "fft_w", "pool_w", "proj_a", "proj_b", "proj_c", "w_out"):
        shared[k] = f(k)
    cctx = np.asarray(inp["c_ctx"], np.float32).reshape(8, 128).T
    maps = []
    for b in cores:
        m = dict(shared)
        m["xT"] = np.ascontiguousarray(np.asarray(inp["x"][b], np.float32).T)
        m["ctxT"] = np.ascontiguousarray(np.asarray(inp["ctx"][b], np.float32).T)
        cb = np.asarray(inp["c"][b], np.float32).reshape(8, 128).T
        m["cc"] = np.ascontiguousarray(np.concatenate([cb, cctx], axis=1))
        maps.append(m)
    return maps


def kernel(**inputs):
    nb = inputs["x"].shape[0]
    nc = build_program()
    maps = _in_maps(inputs, list(range(nb)))
    res = run_bass_kernel_spmd(nc, maps, core_ids=list(range(nb)))
    out = np.stack([np.ascontiguousarray(r["outT"].T) for r in res.results]).astype(np.float32)
    return out
```

```python
from contextlib import ExitStack

import ml_dtypes
import numpy as np

import concourse.bass as bass
import concourse.mybir as mybir
from concourse.bass_utils import run_bass_kernel_spmd

F32 = mybir.dt.float32
BF16 = mybir.dt.bfloat16
AF = mybir.ActivationFunctionType
ALU = mybir.AluOpType

D = 1024
SEQ = 4096
CTX = 256
DEPTH = 2
NT = 4400
C0 = 16
L0 = 288
TILES = [(C0, 256)] + [(L0 + 512 * i, 512) for i in range(8)]
NTI = len(TILES)
INC = 7168
EPS = 1e-6
V_NG, V_AB, V_CW, V_CB, V_BA, V_BX, V_LAM, V_PS = 0, 8, 32, 64, 72, 88, 104, 120
V_L = 124
V_FG = 2 * V_L
NV = V_FG + 8


class Sched:
    def __init__(self, nc, stack):
        self.nc = nc
        self.stack = stack
        self.engs = {}
        for name, e in (("pe", nc.tensor), ("act", nc.scalar), ("dve", nc.vector),
                        ("pool", nc.gpsimd), ("sp", nc.sync)):
            sem = stack.enter_context(nc.semaphore("sem_" + name))
            self.engs[name] = dict(e=e, sem=sem, cnt=0, seen={}, name=name)
        self.dsems = {}
        self.lastw = {}
        self.reads = {}

    def dsem(self, key):
        if key not in self.dsems:
            sem = self.stack.enter_context(self.nc.semaphore("dsem_%d" % len(self.dsems)))
            self.dsems[key] = dict(sem=sem, cnt=0)
        return self.dsems[key]

    def _deps(self, reads, writes):
        deps = []
        for r in reads:
            if r in self.lastw:
                deps.append(self.lastw[r])
        for w in writes:
            if w in self.lastw:
                deps.append(self.lastw[w])
            deps.extend(self.reads.get(w, ()))
        return deps

    def _wait(self, E, deps, skip_self=False):
        best = {}
        for (sem, val, owner) in deps:
            if skip_self and owner == E["name"]:
                continue
            k = id(sem)
            if k not in best or best[k][1] < val:
                best[k] = (sem, val)
        for k, (sem, val) in best.items():
            if E["seen"].get(k, 0) >= val:
                continue
            E["e"].wait_ge(sem, val)
            E["seen"][k] = val

    def _record(self, tok, reads, writes):
        for w in writes:
            self.lastw[w] = tok
            self.reads[w] = []
        for r in reads:
            if r in writes:
                continue
            self.reads.setdefault(r, []).append(tok)

    def op(self, eng, fn, reads=(), writes=(), inc=True):
        E = self.engs[eng]
        self._wait(E, self._deps(reads, writes), skip_self=(eng == "pe"))
        ins = fn()
        if inc:
            ins.then_inc(E["sem"], 1)
            E["cnt"] += 1
            tok = (E["sem"], E["cnt"], eng)
        else:
            tok = (E["sem"], E["cnt"] + 1, eng)
        self._record(tok, reads, writes)
        return ins

    def dma(self, q, out, in_, reads=(), writes=(), key=None, **kw):
        E = self.engs[q]
        self._wait(E, self._deps(reads, writes))
        Dm = self.dsem(key)
        ins = E["e"].dma_start(out=out, in_=in_, **kw)
        ins.then_inc(Dm["sem"], 16)
        Dm["cnt"] += 16
        self._record((Dm["sem"], Dm["cnt"], "dma"), reads, writes)
        return ins

    def barrier(self):
        toks = [(E["sem"], E["cnt"], n) for n, E in self.engs.items() if E["cnt"] > 0]
        toks += [(Dm["sem"], Dm["cnt"], "dma") for Dm in self.dsems.values() if Dm["cnt"] > 0]
        for n, E in self.engs.items():
            self._wait(E, [t for t in toks if t[2] != n])
        self.lastw = {}
        self.reads = {}


def build_program(debug=False):
    nc = bass.Bass("TRN2", target_bir_lowering=False)

    def din(name, shape, dt=F32):
        return nc.dram_tensor(name, list(shape), dt, kind="ExternalInput").ap()

    xT = din("xT", [D, SEQ])
    ctxT = din("ctxT", [D, CTX])
    posT = din("posT", [D, SEQ])
    cc = din("cc", [128, 16])
    vecs_d = din("vecs", [128, NV])
    ada_w = din("ada_w", [DEPTH, D, 3 * D])
    w_in = din("w_in", [DEPTH, D, INC])
    bd_d = din("bd", [DEPTH, 8, 128, 4, 128])
    fftw_d = din("fft_w", [DEPTH, 4, 128, 128])
    poolw_d = din("pool_w", [DEPTH, 4, 128, 128])
    proj_a = din("proj_a", [DEPTH, D, D])
    proj_b = din("proj_b", [DEPTH, 512, D])
    proj_c = din("proj_c", [DEPTH, 512, D])
    w_out = din("w_out", [DEPTH, D, D])
    ident_d = din("ident", [128, 128])
    edge_d = din("edge", [128, 64])
    cs128_d = din("cs128", [128, 256], BF16)
    cs256_d = din("cs256", [128, 2, 2, 256], BF16)
    dft_d = din("dft", [4, 2, 2, 128, 8, 512], BF16)
    cs4_d = din("cs4", [128, 4, 4, 256], BF16)
    okind = "ExternalOutput" if debug else "Internal"
    XS = [nc.dram_tensor("xs%d" % l, [D, NT], F32, kind=okind).ap() for l in range(DEPTH)]
    GT = [nc.dram_tensor("gt%d" % l, [2048, NT], BF16, kind=okind).ap() for l in range(DEPTH)]
    outT = nc.dram_tensor("outT", [D, SEQ], F32, kind="ExternalOutput").ap()
    WBS = [nc.dram_tensor("wbs%d" % l, [3072, D], BF16, kind="Internal").ap() for l in range(DEPTH)]
    WGS = [nc.dram_tensor("wgs%d" % l, [D, 3 * D], BF16, kind="Internal").ap() for l in range(DEPTH)]

    with ExitStack() as st:
        S = Sched(nc, st)

        uid = [0]

        def sb(stack, name, shape, dt=F32):
            uid[0] += 1
            return stack.enter_context(nc.sbuf_tensor("%s_%d" % (name, uid[0]), list(shape), dt))

        PS = [st.enter_context(nc.psum_tensor("ps%d" % i, [128, 512], F32)) for i in range(8)]
        bank_rr = [0]

        def bank():
            b = bank_rr[0]
            bank_rr[0] = (b + 1) % 8
            return b

        def mm(b, n, pairs, reads, c_off=0):
            k = len(pairs)
            for i, (l_, r_) in enumerate(pairs):
                S.op("pe", lambda l_=l_, r_=r_, i=i: nc.tensor.matmul(
                    PS[b][:, c_off:c_off + n], lhsT=l_, rhs=r_, start=(i == 0), stop=(i == k - 1)),
                    reads=(reads if i == 0 else ()), writes=[("ps", b)], inc=(i == k - 1))

        VEC = sb(st, "vec", [128, NV])
        CCs = sb(st, "ccs", [128, 8, 2])
        MOD = sb(st, "mod", [128, DEPTH, 24, 2])
        GM = sb(st, "gm", [128, DEPTH, 8, 2])
        CNEG = sb(st, "cneg", [128, DEPTH, 16])
        CNEG2 = sb(st, "cneg2", [128, DEPTH, 16])
        ONES = sb(st, "ones", [128, 128], BF16)
        IDENT = sb(st, "identt", [128, 128])
        EDGE = sb(st, "edget", [128, 64])
        CS128 = sb(st, "cs128t", [128, 256], BF16)
        CS256 = sb(st, "cs256t", [128, 2, 2, 256], BF16)

        S.dma("sp", VEC[:], vecs_d, writes=["VEC"], key="c0")
        S.dma("sp", IDENT[:], ident_d, writes=["IDENT"], key="c1")
        S.dma("sp", EDGE[:], edge_d, writes=["EDGE"], key="c2")
        S.dma("sp", CS128[:], cs128_d, writes=["CS128"], key="c3")
        S.dma("sp", CS256[:], cs256_d, writes=["CS256"], key="c4")
        S.op("dve", lambda: nc.vector.memset(ONES[:], 1.0), writes=["ONES"])

        def vcol(l, base, i):
            c = l * V_L + base + i
            return VEC[:, c:c + 1]

        ada_piece_bufs = [None]

        def ada_piece(l, pc):
            AW = ada_piece_bufs[0]
            bi = pc % 2
            S.dma("sp", AW[bi][:], ada_w[l].rearrange("(kc p) c -> p kc c", p=128)[:, :, 512 * pc:512 * pc + 512],
                  writes=[("AW", bi)], key=("aw", bi))
            b = bank()
            for q in range(4):
                mm(b, 2, [(AW[bi][:, kc, 128 * q:128 * q + 128], CCs[:, kc, :]) for kc in range(8)],
                   reads=[("AW", bi), "CCs"], c_off=2 * q)
            S.op("dve", lambda: nc.vector.tensor_copy(
                out=MOD[:, l, 4 * pc:4 * pc + 4, :], in_=PS[b][:, 0:8].rearrange("p (a b) -> p a b", b=2)),
                reads=[("ps", b)], writes=[("MOD", l)])

        def ada_finish(l):
            for s_ in range(2):
                S.op("dve", lambda s_=s_: nc.vector.tensor_tensor(
                    out=MOD[:, l, :, s_], in0=MOD[:, l, :, s_], in1=VEC[:, l * V_L + V_AB:l * V_L + V_AB + 24], op=ALU.add),
                    reads=[("MOD", l), "VEC"], writes=[("MOD", l)])
                S.op("dve", lambda s_=s_: nc.vector.scalar_tensor_tensor(
                    out=GM[:, l, :, s_], in0=MOD[:, l, 8:16, s_], scalar=1.0, in1=VEC[:, l * V_L + V_NG:l * V_L + V_NG + 8],
                    op0=ALU.add, op1=ALU.mult), reads=[("MOD", l), "VEC"], writes=[("GM", l)])

        with ExitStack() as s0:
            CCr = sb(s0, "ccr", [128, 16])
            AW = [sb(s0, "aw%d" % i, [128, 8, 512]) for i in range(2)]
            T1 = sb(s0, "t1", [128, 32])
            T2 = sb(s0, "t2", [128, 32])
            T3 = sb(s0, "t3", [128, 32])
            S.dma("sp", CCr[:], cc, writes=["CCr"], key="c5")
            for s_ in range(2):
                S.op("act", lambda s_=s_: nc.scalar.activation(out=CCs[:, :, s_], in_=CCr[:, 8 * s_:8 * s_ + 8], func=AF.Silu),
                     reads=["CCr"], writes=["CCs"])
            ada_piece_bufs[0] = AW
            for pc in range(6):
                ada_piece(0, pc)
            ada_finish(0)
            for l in range(DEPTH):
                lam = VEC[:, l * V_L + V_LAM:l * V_L + V_LAM + 16]
                e_ = T1[:, 0:16]
                ln_ = T1[:, 16:32]
                ser = T2[:, 0:16]
                msk = T2[:, 16:32]
                tmp = T3[:, 0:16]
                S.op("act", lambda: nc.scalar.activation(out=e_, in_=lam, func=AF.Exp, scale=-1.0), reads=["VEC", "CNEG"], writes=["T1"])
                S.op("act", lambda: nc.scalar.activation(out=ln_, in_=e_, func=AF.Ln, bias=1.0, scale=1.0), reads=["T1"], writes=["T1b"])
                S.op("dve", lambda: nc.vector.tensor_scalar(out=ser, in0=e_, scalar1=-0.25, scalar2=1.0 / 3.0, op0=ALU.mult, op1=ALU.add), reads=["T1"], writes=["T2"])
                S.op("dve", lambda: nc.vector.tensor_tensor(out=ser, in0=ser, in1=e_, op=ALU.mult), reads=["T2", "T1"], writes=["T2"])
                S.op("dve", lambda: nc.vector.tensor_scalar(out=ser, in0=ser, scalar1=-0.5, scalar2=None, op0=ALU.add), reads=["T2"], writes=["T2"])
                S.op("dve", lambda: nc.vector.tensor_tensor(out=ser, in0=ser, in1=e_, op=ALU.mult), reads=["T2", "T1"], writes=["T2"])
                S.op("dve", lambda: nc.vector.tensor_scalar(out=ser, in0=ser, scalar1=1.0, scalar2=None, op0=ALU.add), reads=["T2"], writes=["T2"])
                S.op("dve", lambda: nc.vector.tensor_tensor(out=ser, in0=ser, in1=e_, op=ALU.mult), reads=["T2", "T1"], writes=["T2"])
                S.op("dve", lambda: nc.vector.tensor_scalar(out=msk, in0=e_, scalar1=0.1, scalar2=None, op0=ALU.is_lt), reads=["T1"], writes=["T2m"])
                S.op("dve", lambda: nc.vector.tensor_tensor(out=tmp, in0=ser, in1=ln_, op=ALU.subtract), reads=["T2", "T1b"], writes=["T3"])
                S.op("dve", lambda: nc.vector.tensor_tensor(out=tmp, in0=tmp, in1=msk, op=ALU.mult), reads=["T3", "T2m"], writes=["T3"])
                S.op("dve", lambda: nc.vector.tensor_tensor(out=tmp, in0=tmp, in1=ln_, op=ALU.add), reads=["T3", "T1b"], writes=["T3"])
                S.op("dve", lambda l=l: nc.vector.tensor_scalar(out=CNEG[:, l, :], in0=tmp, scalar1=-8.0, scalar2=None, op0=ALU.mult), reads=["T3"], writes=["CNEG"])
                S.op("dve", lambda l=l: nc.vector.tensor_scalar(out=CNEG2[:, l, :], in0=tmp, scalar1=-4.0, scalar2=None, op0=ALU.mult), reads=["T3"], writes=["CNEG"])
            S.barrier()

        def load_w(dst, src2d, key, reskey):
            S.dma("pool", dst, src2d.rearrange("(kc p) c -> p kc c", p=128), writes=[reskey], key=key)

        def rms_tile(XTb, xkey, n, gm_ap, sh_ap, out_fn, SQ, RS, TMP, okeys, phase="all", rskey="RS"):
            if phase != "b":
                rms_stats(XTb, xkey, n, SQ, RS, rskey)
            if phase != "a":
                rms_apply(XTb, xkey, n, gm_ap, sh_ap, out_fn, RS, TMP, okeys, rskey)

        def rms_stats(XTb, xkey, n, SQ, RS, rskey):
            b = bank()
            for hf in range(2):
                S.op("act", lambda hf=hf: nc.scalar.activation(out=SQ[:, :, 0:n], in_=XTb[:, 4 * hf:4 * hf + 4, 0:n], func=AF.Square),
                     reads=[xkey], writes=["SQ"])
                for q in range(4):
                    S.op("pe", lambda q=q, hf=hf: nc.tensor.matmul(PS[b][:, 0:n], lhsT=ONES[:], rhs=SQ[:, q, 0:n], start=(hf == 0 and q == 0),
                                                                   stop=(hf == 1 and q == 3)),
                         reads=(["ONES", "SQ"] if q == 0 else ()), writes=[("ps", b)], inc=(q == 3))
                S._record((S.engs["pe"]["sem"], S.engs["pe"]["cnt"], "pe"), ["SQ"], [])
            S.op("act", lambda: nc.scalar.activation(out=RS[:, 0:n], in_=PS[b][:, 0:n], func=AF.Sqrt, scale=1.0 / D, bias=EPSC[:, 0:1]),
                 reads=[("ps", b), "EPSC"], writes=[rskey])
            S.op("dve", lambda: nc.vector.reciprocal(out=RS[:, 0:n], in_=RS[:, 0:n]), reads=[rskey], writes=[rskey])

        def rms_apply(XTb, xkey, n, gm_ap, sh_ap, out_fn, RS, TMP, okeys, rskey):
            for fc in range(8):
                tb = fc % 2
                S.op("dve", lambda fc=fc, tb=tb: nc.vector.tensor_tensor(out=TMP[:, tb, 0:n], in0=XTb[:, fc, 0:n], in1=RS[:, 0:n], op=ALU.mult),
                     reads=[xkey, rskey], writes=[("TMP", tb)])
                if sh_ap is None:
                    S.op("act", lambda fc=fc, tb=tb: nc.scalar.activation(out=out_fn(fc), in_=TMP[:, tb, 0:n], func=AF.Identity, scale=gm_ap(fc)),
                         reads=[("TMP", tb), "GM", "VEC"], writes=okeys)
                else:
                    S.op("act", lambda fc=fc, tb=tb: nc.scalar.activation(out=out_fn(fc), in_=TMP[:, tb, 0:n], func=AF.Identity, scale=gm_ap(fc), bias=sh_ap(fc)),
                         reads=[("TMP", tb), "GM", "MOD"], writes=okeys)

        Q16 = sb(st, "q16", [128, 1])
        S.op("dve", lambda: nc.vector.memset(Q16[:], 1.0 / 16), writes=["Q16"])
        EPSC = sb(st, "epsc", [128, 1])
        S.op("dve", lambda: nc.vector.memset(EPSC[:], EPS), writes=["EPSC"])

        for l in range(DEPTH):
            last = (l == DEPTH - 1)
            with ExitStack() as sl:
                HT = sb(sl, "ht", [128, 8, NT], BF16)
                with ExitStack() as s0:
                    XTs = [sb(s0, "xt%d" % i, [128, 8, 512]) for i in range(3)]
                    PTs = [sb(s0, "pt%d" % i, [128, 8, 512]) for i in range(2)]
                    SQ = sb(s0, "sq", [128, 4, 512], BF16)
                    RSs = [sb(s0, "rs%d" % i, [128, 512]) for i in range(2)]
                    TMP = sb(s0, "tmp", [128, 2, 512])
                    if l == 0 and DEPTH > 1:
                        ada_piece_bufs[0] = [sb(s0, "aw2_%d" % i, [128, 8, 512]) for i in range(2)]

                    def p0_a(ti):
                        c0, n = TILES[ti]
                        bi = ti % 3
                        pb = ti % 2
                        XTb = XTs[bi]
                        xkey = ("XT", bi)
                        if l == 0:
                            if ti == 0:
                                S.dma("sp", XTb[:, :, 0:n], ctxT.rearrange("(fc p) t -> p fc t", p=128), writes=[xkey], key=("xt", bi))
                            else:
                                t0 = c0 - L0
                                S.dma("sp", XTb[:, :, 0:n], xT.rearrange("(fc p) t -> p fc t", p=128)[:, :, t0:t0 + n], writes=[xkey], key=("xt", bi))
                                S.dma("sp", PTs[pb][:, :, 0:n], posT.rearrange("(fc p) t -> p fc t", p=128)[:, :, t0:t0 + n], writes=[("PT", pb)], key=("pt", pb))
                                S.op("pool", lambda: nc.gpsimd.tensor_tensor(out=XTb[:], in0=XTb[:], in1=PTs[pb][:], op=ALU.add),
                                     reads=[xkey, ("PT", pb)], writes=[xkey])
                            S.dma("sp", XS[0].rearrange("(fc p) t -> p fc t", p=128)[:, :, c0:c0 + n], XTb[:, :, 0:n], reads=[xkey], writes=[("XS", ti)], key=("xso", bi))
                        else:
                            S.dma("sp", XTb[:, :, 0:n], XS[l].rearrange("(fc p) t -> p fc t", p=128)[:, :, c0:c0 + n], writes=[xkey], key=("xt", bi))
                        rms_stats(XTb, xkey, n, SQ, RSs[pb], ("RS", pb))

                    def p0_b(ti):
                        c0, n = TILES[ti]
                        bi = ti % 3
                        pb = ti % 2
                        s_ = 1 if ti == 0 else 0
                        rms_apply(XTs[bi], ("XT", bi), n, lambda fc: GM[:, l, fc, s_:s_ + 1], lambda fc: MOD[:, l, fc, s_:s_ + 1],
                                  lambda fc: HT[:, fc, c0:c0 + n], RSs[pb], TMP, [("HT", ti)], ("RS", pb))

                    p0_a(0)
                    for ti in range(NTI):
                        if l == 0 and DEPTH > 1 and 1 <= ti <= 6:
                            ada_piece(1, ti - 1)
                            if ti == 6:
                                ada_finish(1)
                        if ti + 1 < NTI:
                            p0_a(ti + 1)
                        p0_b(ti)
                    S.barrier()

                HTK = [("HT", ti) for ti in range(NTI)]

                def slab_proj(WT, wkey, consume):
                    for ti, (c0, n) in enumerate(TILES):
                        b = bank()
                        mm(b, n, [(WT[:, kc, :], HT[:, kc, c0:c0 + n]) for kc in range(8)], reads=[wkey, ("HT", ti)])
                        consume(ti, c0, n, b)

                with ExitStack() as sa:
                    XCb = sb(sa, "xcb", [128, NT], BF16)
                    G = sb(sa, "g", [128, NT], BF16)
                    XC = sb(sa, "xc", [128, NT])
                    WU = [sb(sa, "wu%d" % i, [128, 8, 128], BF16) for i in range(2)]
                    WZ = [sb(sa, "wz%d" % i, [128, 8, 128], BF16) for i in range(2)]
                    PW = sb(sa, "pw", [128, 4, 128], BF16)
                    ZS = [sb(sa, "zs%d" % i, [128, 512]) for i in range(2)]
                    DAT = slice(C0, L0 + SEQ)

                    def tk(name):
                        return [(name, ti) for ti in range(NTI)]

                    S.dma("pool", PW[:], poolw_d[l].rearrange("g i j -> i g j"), writes=["PW"], key="pw")

                    def issue_w(slab_idx):
                        bi = slab_idx % 2
                        if slab_idx < 8:
                            j = slab_idx
                            load_w(WU[bi][:], w_in[l][:, 128 * j:128 * j + 128], ("wu", bi), ("WU", bi))
                            load_w(WZ[bi][:], w_in[l][:, 1024 + 128 * j:1024 + 128 * j + 128], ("wz", bi), ("WZ", bi))
                            S.dma("pool", BD[bi][:], bd_d[l, j], writes=[("BD", bi)], key=("bd", bi))
                        else:
                            g = slab_idx - 8
                            load_w(WU[bi][:], w_in[l][:, 3072 + 128 * g:3072 + 128 * g + 128], ("wu", bi), ("WU", bi))
                            load_w(WZ[bi][:], w_in[l][:, 3584 + 128 * g:3584 + 128 * g + 128], ("wz", bi), ("WZ", bi))

                    def gate_out(slab, y_ap_fn, ykeys, scale_ap=None):
                        bi = slab % 2 if slab < 8 else (slab - 8) % 2
                        if slab >= 12:
                            bi = (slab - 4) % 2

                        def cons(ti, c0, n, b):
                            zb = ti % 2
                            S.op("act", lambda: nc.scalar.activation(out=ZS[zb][:, 0:n], in_=PS[b][:, 0:n], func=AF.Silu),
                                 reads=[("ps", b)], writes=[("ZS", zb)])
                            yap, ykey = y_ap_fn(ti, c0, n)
                            if scale_ap is None:
                                S.op("dve", lambda: nc.vector.tensor_tensor(out=G[:, c0:c0 + n], in0=yap, in1=ZS[zb][:, 0:n], op=ALU.mult),
                                     reads=[("ZS", zb)] + ykey, writes=[("G", ti)])
                            else:
                                S.op("dve", lambda: nc.vector.scalar_tensor_tensor(out=G[:, c0:c0 + n], in0=yap, scalar=scale_ap, in1=ZS[zb][:, 0:n],
                                                                                   op0=ALU.mult, op1=ALU.mult),
                                     reads=[("ZS", zb), "VEC"] + ykey, writes=[("G", ti)])
                        return cons

                    with ExitStack() as sr:
                        U = sb(sr, "u", [128, NT], BF16)
                        HF = sb(sr, "hf", [128, NT])
                        RW = 1536
                        RA = [sb(sr, "ra%d" % i, [128, RW]) for i in range(2)]
                        RSq = [sb(sr, "rs%d" % i, [128, RW]) for i in range(2)]
                        RI = [sb(sr, "ri%d" % i, [128, RW]) for i in range(2)]
                        RH = [sb(sr, "rh%d" % i, [128, RW]) for i in range(2)]
                        TZ = [sb(sr, "tz%d" % i, [128, 512]) for i in range(2)]
                        BD = [sb(sr, "bd%d" % i, [128, 4, 128], BF16) for i in range(2)]
                        DGs = [sb(sr, "dg%d" % i, [128, 4, 128], BF16) for i in range(2)]
                        HB5 = sb(sr, "hb5", [128, 32])
                        S.op("pool", lambda: nc.gpsimd.memset(U[:], 0.0), writes=tk("U") + [("U", "pad")])
                        S.op("dve", lambda: nc.vector.tensor_scalar(out=HB5[:], in0=VEC[:, l * V_L + V_BA:l * V_L + V_BA + 32], scalar1=0.5, scalar2=None, op0=ALU.mult),
                             reads=["VEC"], writes=["HB5"])
                        FWD = [(C0, 1312, [0, 1, 2]), (1312, 2848, [3, 4, 5]), (2848, 4384, [6, 7, 8])]
                        MLO = 3360
                        BWD = [("M", None, [0, 8, 7]), (1824, 3360, [6, 5, 4]), (L0, 1824, [3, 2, 1])]

                        def ring_off(g0, t):
                            if g0 == "M":
                                return 0 if t == 0 else 256 + TILES[t][0] - MLO
                            return TILES[t][0] - g0

                        def ring_w(g0, g1):
                            return 1280 if g0 == "M" else g1 - g0
                        gcount = [0]
                        def uproj_tile(jn, ti):
                            c0, n = TILES[ti]
                            b = bank()
                            mm(b, n, [(WU[jn % 2][:, kc, :], HT[:, kc, c0:c0 + n]) for kc in range(8)], reads=[("WU", jn % 2), ("HT", ti)])
                            S.op("dve", lambda: nc.vector.tensor_copy(out=U[:, c0:c0 + n], in_=PS[b][:, 0:n]), reads=[("ps", b)], writes=[("U", ti)])

                        issue_w(0)
                        for ti in range(NTI):
                            uproj_tile(0, ti)
                        for j in range(8):
                            bi = j % 2
                            issue_w(j + 1)
                            upend = list(range(NTI)) if j + 1 < 8 else []
                            DG = DGs[bi]
                            for tap in range(4):
                                S.op("dve", lambda tap=tap, j=j: nc.vector.tensor_scalar(
                                    out=DG[:, tap, :], in0=IDENT[:], scalar1=vcol(l, V_CW, tap * 8 + j), scalar2=None, op0=ALU.mult),
                                    reads=["IDENT", "VEC"], writes=[("DG", bi)])

                            for ti, (c0, n) in enumerate(TILES):
                                b = bank()
                                mm(b, n, [(DG[:, tap, :], U[:, c0 + tap - 2:c0 + tap - 2 + n]) for tap in range(4)],
                                   reads=[("DG", bi), ("U", "pad")] + tk("U"))
                                S.op("dve", lambda c0=c0, n=n, b=b: nc.vector.tensor_scalar(out=XC[:, c0:c0 + n], in0=PS[b][:, 0:n], scalar1=vcol(l, V_CB, j),
                                                                                           scalar2=None, op0=ALU.add),
                                     reads=[("ps", b), "VEC"], writes=[("XC", ti)])
                                S.op("dve", lambda c0=c0, n=n, b=b: nc.vector.tensor_scalar(out=XCb[:, c0:c0 + n], in0=PS[b][:, 0:n], scalar1=vcol(l, V_CB, j),
                                                                                           scalar2=None, op0=ALU.add),
                                     reads=[("ps", b), "VEC"], writes=[("XCb", ti)])
                            for d in range(2):
                                groups = FWD if d == 0 else BWD
                                cidx = d * 8 + j
                                c_full = CNEG[:, l, cidx:cidx + 1]
                                c_half = CNEG2[:, l, cidx:cidx + 1]
                                pend = None

                                def stage2(gi, g0, g1, tl, rs):
                                    w = ring_w(g0, g1)
                                    rk = [("RA", rs), ("RI", rs)]
                                    if d == 0:
                                        if gi == 0:
                                            S.op("dve", lambda: nc.vector.tensor_tensor_scan(out=HF[:, C0:C0 + CTX], data0=RA[rs][:, 0:CTX], data1=RI[rs][:, 0:CTX],
                                                                                             initial=0.0, op0=ALU.mult, op1=ALU.add),
                                                 reads=rk, writes=[("HF", 0)])
                                            o = L0 - g0
                                            S.op("dve", lambda: nc.vector.tensor_tensor_scan(out=HF[:, L0:g1], data0=RA[rs][:, o:w], data1=RI[rs][:, o:w],
                                                                                             initial=HF[:, C0 + CTX - 1:C0 + CTX], op0=ALU.mult, op1=ALU.add),
                                                 reads=rk + [("HF", 0)], writes=[("HF", t) for t in tl[1:]])
                                        else:
                                            S.op("dve", lambda: nc.vector.tensor_tensor_scan(out=HF[:, g0:g1], data0=RA[rs][:, 0:w], data1=RI[rs][:, 0:w],
                                                                                             initial=HF[:, g0 - 1:g0], op0=ALU.mult, op1=ALU.add),
                                                 reads=rk + tk("HF"), writes=[("HF", t) for t in tl])
                                        return
                                    if g0 == "M":
                                        S.op("dve", lambda: nc.vector.tensor_tensor_scan(out=RH[rs][:, 255::-1], data0=RA[rs][:, 255::-1], data1=RI[rs][:, 255::-1],
                                                                                         initial=0.0, op0=ALU.mult, op1=ALU.add),
                                             reads=rk + [("RH", 1 - rs)], writes=[("RH", rs)])
                                        S.op("dve", lambda: nc.vector.tensor_tensor_scan(out=RH[rs][:, 1279:255:-1], data0=RA[rs][:, 1279:255:-1], data1=RI[rs][:, 1279:255:-1],
                                                                                         initial=RH[rs][:, 0:1], op0=ALU.mult, op1=ALU.add),
                                             reads=rk + [("RH", rs)], writes=[("RH", rs)])
                                        S.op("pool", lambda: nc.gpsimd.tensor_tensor(out=RI[rs][:, 0:256], in0=RH[rs][:, 0:256], in1=HF[:, C0:C0 + CTX], op=ALU.add),
                                             reads=[("RH", rs)] + [("HF", t) for t in tl], writes=[("RI", rs)])
                                        S.op("pool", lambda: nc.gpsimd.tensor_tensor(out=RI[rs][:, 256:1280], in0=RH[rs][:, 256:1280], in1=HF[:, MLO:MLO + 1024], op=ALU.add),
                                             reads=[("RH", rs)] + [("HF", t) for t in tl], writes=[("RI", rs)])
                                    else:
                                        co = 256 if gi == 1 else 0
                                        init = RH[1 - rs][:, co:co + 1]
                                        S.op("dve", lambda: nc.vector.tensor_tensor_scan(out=RH[rs][:, w - 1::-1], data0=RA[rs][:, w - 1::-1], data1=RI[rs][:, w - 1::-1],
                                                                                         initial=init, op0=ALU.mult, op1=ALU.add),
                                             reads=rk + [("RH", 1 - rs)], writes=[("RH", rs)])
                                        S.op("pool", lambda: nc.gpsimd.tensor_tensor(out=RI[rs][:, 0:w], in0=RH[rs][:, 0:w], in1=HF[:, g0:g1], op=ALU.add),
                                             reads=[("RH", rs)] + [("HF", t) for t in tl], writes=[("RI", rs)])
                                    for t in tl:
                                        c0, n = TILES[t]
                                        zb = t % 2
                                        b = bank()
                                        mm(b, n, [(WZ[bi][:, kc, :], HT[:, kc, c0:c0 + n]) for kc in range(8)], reads=[("WZ", bi), ("HT", t)])
                                        S.op("act", lambda b=b, n=n, zb=zb: nc.scalar.activation(out=TZ[zb][:, 0:n], in_=PS[b][:, 0:n], func=AF.Tanh, scale=0.5),
                                             reads=[("ps", b)], writes=[("TZ", zb)])
                                        S.op("dve", lambda b=b, n=n, zb=zb: nc.vector.scalar_tensor_tensor(out=TZ[zb][:, 0:n], in0=TZ[zb][:, 0:n], scalar=1.0, in1=PS[b][:, 0:n],
                                                                                                        op0=ALU.add, op1=ALU.mult),
                                             reads=[("ps", b), ("TZ", zb)], writes=[("TZ", zb)])
                                        S.op("pool", lambda c0=c0, n=n, zb=zb: nc.gpsimd.tensor_tensor(out=G[:, c0:c0 + n], in0=RI[rs][:, ring_off(g0, t):ring_off(g0, t) + n], in1=TZ[zb][:, 0:n], op=ALU.mult),
                                             reads=[("RI", rs), ("TZ", zb)], writes=[("G", t)])

                                for gi, (g0, g1, tl) in enumerate(groups):
                                    rs = gcount[0] % 2
                                    gcount[0] += 1
                                    w = ring_w(g0, g1)
                                    for t in tl:
                                        c0, n = TILES[t]
                                        o = ring_off(g0, t)
                                        b = bank()
                                        mm(b, n, [(BD[bi][:, 2 * d, :], XCb[:, c0:c0 + n])], reads=[("BD", bi), ("XCb", t)])
                                        S.op("act", lambda o=o, n=n, b=b: nc.scalar.activation(out=RA[rs][:, o:o + n], in_=PS[b][:, 0:n], func=AF.Tanh,
                                                                                               bias=HB5[:, cidx:cidx + 1], scale=0.5),
                                             reads=[("ps", b), "HB5"], writes=[("RA", rs)])
                                        b2 = bank()
                                        mm(b2, n, [(BD[bi][:, 2 * d + 1, :], XCb[:, c0:c0 + n])], reads=[("BD", bi), ("XCb", t)])
                                        S.op("act", lambda o=o, n=n, b2=b2: nc.scalar.activation(out=RI[rs][:, o:o + n], in_=PS[b2][:, 0:n], func=AF.Tanh,
                                                                                                 bias=HB5[:, 16 + cidx:16 + cidx + 1], scale=0.5),
                                             reads=[("ps", b2), "HB5"], writes=[("RI", rs)])
                                    S.op("act", lambda: nc.scalar.activation(out=RSq[rs][:, 0:w], in_=RA[rs][:, 0:w], func=AF.Exp, scale=c_full, bias=c_full),
                                         reads=[("RA", rs), "CNEG"], writes=[("RS", rs)])
                                    S.op("act", lambda: nc.scalar.activation(out=RA[rs][:, 0:w], in_=RA[rs][:, 0:w], func=AF.Exp, scale=c_half, bias=c_half),
                                         reads=[("RA", rs), "CNEG"], writes=[("RA", rs)])
                                    S.op("act", lambda: nc.scalar.activation(out=RSq[rs][:, 0:w], in_=RSq[rs][:, 0:w], func=AF.Sqrt, scale=-1.0 / 16, bias=Q16[:, 0:1]),
                                         reads=[("RS", rs), "Q16"], writes=[("RS", rs)])
                                    for (rl, rh_, cl) in ([(0, 256, C0), (256, 1280, MLO)] if g0 == "M" else [(0, w, g0)]):
                                        S.op("dve", lambda rl=rl, rh_=rh_, cl=cl: nc.vector.scalar_tensor_tensor(
                                            out=RI[rs][:, rl:rh_], in0=RI[rs][:, rl:rh_], scalar=1.0, in1=XC[:, cl:cl + rh_ - rl], op0=ALU.add, op1=ALU.mult),
                                             reads=[("RI", rs)] + [("XC", t) for t in tl], writes=[("RI", rs)])
                                    S.op("pool", lambda: nc.gpsimd.tensor_tensor(out=RI[rs][:, 0:w], in0=RI[rs][:, 0:w], in1=RSq[rs][:, 0:w], op=ALU.mult),
                                         reads=[("RI", rs), ("RS", rs)], writes=[("RI", rs)])
                                    if pend is not None:
                                        stage2(*pend)
                                    pend = (gi, g0, g1, tl, rs)
                                    for _ in range(2):
                                        if upend:
                                            uproj_tile(j + 1, upend.pop(0))
                                stage2(*pend)
                            while upend:
                                uproj_tile(j + 1, upend.pop(0))
                            S.dma("sp", GT[l][128 * j:128 * j + 128, :], G[:], reads=tk("G"), writes=[("GT", j)], key="gst")
                        S.barrier()

                    A = sb(sa, "a", [128, NT])
                    Sb = sb(sa, "s", [128, NT])
                    IB = sb(sa, "ib", [128, NT])
                    S.op("pool", lambda: nc.gpsimd.memset(XC[:], 0.0), reads=tk("XC"), writes=tk("XC") + [("XC", "pad")])
                    for g in range(4):
                        sidx = 8 + g
                        bi = sidx % 2
                        if g < 3:
                            issue_w(sidx + 1)
                        win = 2 << g

                        def cons_up(ti, c0, n, b):
                            S.op("act", lambda: nc.scalar.activation(out=XC[:, c0:c0 + n], in_=PS[b][:, 0:n], func=AF.Identity),
                                 reads=[("ps", b)], writes=[("XC", ti)])
                        slab_proj(WU[bi], ("WU", bi), cons_up)
                        allx = tk("XC") + [("XC", "pad")]
                        S.op("dve", lambda: nc.vector.tensor_tensor(out=A[:, 9:NT - 9], in0=XC[:, 8:NT - 10], in1=XC[:, 9:NT - 9], op=ALU.add),
                             reads=allx + tk("A"), writes=tk("A"))
                        cur, curk = A, "A"
                        if g >= 1:
                            S.op("dve", lambda: nc.vector.tensor_tensor(out=Sb[:, 10:NT - 10], in0=A[:, 9:NT - 11], in1=A[:, 11:NT - 9], op=ALU.add),
                                 reads=tk("A") + tk("S"), writes=tk("S"))
                            cur, curk = Sb, "S"
                        if g >= 2:
                            S.op("dve", lambda: nc.vector.tensor_tensor(out=A[:, 12:NT - 12], in0=Sb[:, 10:NT - 14], in1=Sb[:, 14:NT - 10], op=ALU.add),
                                 reads=tk("S") + tk("A"), writes=tk("A"))
                            cur, curk = A, "A"
                        if g >= 3:
                            S.op("dve", lambda: nc.vector.tensor_tensor(out=Sb[:, 16:NT - 16], in0=A[:, 12:NT - 20], in1=A[:, 20:NT - 12], op=ALU.add),
                                 reads=tk("A") + tk("S"), writes=tk("S"))
                            cur, curk = Sb, "S"
                        S.op("dve", lambda cur=cur: nc.vector.scalar_tensor_tensor(out=XCb[:, DAT], in0=cur[:, DAT], scalar=1.0 / win, in1=XC[:, DAT],
                                                                                   op0=ALU.mult, op1=ALU.subtract),
                             reads=tk(curk) + allx + tk("XCb"), writes=tk("XCb"))
                        hw = win // 2
                        for (s0_, Ls) in ((C0, CTX), (L0, SEQ)):
                            for side in range(2):
                                ncol = hw if side == 0 else hw - 1
                                if ncol == 0:
                                    continue
                                cst = s0_ if side == 0 else s0_ + Ls - ncol
                                tb = EDGE[:, g * 16 + side * 8:g * 16 + side * 8 + ncol]
                                S.op("dve", lambda cur=cur, cst=cst, ncol=ncol, tb=tb: nc.vector.tensor_tensor(
                                    out=IB[:, cst:cst + ncol], in0=cur[:, cst:cst + ncol], in1=tb, op=ALU.mult),
                                    reads=tk(curk) + ["EDGE"] + tk("IB"), writes=tk("IB"))
                                S.op("dve", lambda cst=cst, ncol=ncol: nc.vector.tensor_tensor(
                                    out=XCb[:, cst:cst + ncol], in0=IB[:, cst:cst + ncol], in1=XC[:, cst:cst + ncol], op=ALU.subtract),
                                    reads=tk("IB") + allx + tk("XCb"), writes=tk("XCb"))
                        ybanks = {}

                        def y_fn(ti, c0, n):
                            b = bank()
                            mm(b, n, [(PW[:, g, :], XCb[:, c0:c0 + n])], reads=["PW", ("XCb", ti)])
                            return PS[b][:, 0:n], [("ps", b)]
                        slab_proj(WZ[bi], ("WZ", bi), gate_out(12 + g, y_fn, None, scale_ap=vcol(l, V_PS, g)))
                        S.dma("sp", GT[l][128 * (12 + g):128 * (12 + g) + 128, :], G[:], reads=tk("G"), writes=[("GT", 12 + g)], key="gst")
                    S.barrier()

                with ExitStack() as sf:
                    PQc = sb(sf, "pqc", [128, 2, 4, 256], BF16)
                    ZPQ = sb(sf, "zpq", [128, 4, 8, 4, 256], BF16)
                    WZF = sb(sf, "wzf", [128, 4, 8, 128], BF16)
                    FW = sb(sf, "fw", [128, 4, 128], BF16)
                    CS4 = sb(sf, "cs4", [128, 4, 4, 256], BF16)
                    ZS = [sb(sf, "zsf%d" % i, [128, 512]) for i in range(2)]
                    GS = [sb(sf, "gs%d" % i, [128, 512], BF16) for i in range(2)]
                    S.dma("sp", CS4[:], cs4_d, writes=["CS4"], key="cs4")
                    S.dma("pool", FW[:], fftw_d[l].rearrange("g i j -> i g j"), writes=["FW"], key="fw")
                    for g in range(4):
                        load_w(WZF[:, g], w_in[l][:, 2560 + 128 * g:2560 + 128 * g + 128], ("wzf", g), ("WZF", g))
                    with ExitStack() as sf1:
                        UF = sb(sf1, "uf", [128, NT], BF16)
                        WUF = [sb(sf1, "wuf%d" % i, [128, 8, 128], BF16) for i in range(2)]
                        load_w(WUF[0][:], w_in[l][:, 2048:2048 + 128], ("wuf", 0), ("WUF", 0))
                        for g in range(4):
                            bi = g % 2
                            if g < 3:
                                load_w(WUF[1 - bi][:], w_in[l][:, 2048 + 128 * (g + 1):2048 + 128 * (g + 2)], ("wuf", 1 - bi), ("WUF", 1 - bi))
                            if g == 2:
                                S.dma("pool", WBS[l][0:1024, :], proj_a[l], writes=["WBS"], key="wbs")
                                S.dma("pool", WBS[l][1024:1536, :], proj_b[l], writes=["WBS"], key="wbs")
                                S.dma("pool", WBS[l][1536:2048, :], proj_c[l], writes=["WBS"], key="wbs")
                                S.dma("pool", WBS[l][2048:3072, :], w_out[l], writes=["WBS"], key="wbs")
                                for h in range(2):
                                    S.dma("pool", WGS[l][:, 1536 * h:1536 * h + 1536], w_in[l][:, 4096 + 1536 * h:4096 + 1536 * h + 1536], writes=["WGS"], key="wgs")

                            def cons_uf(ti, c0, n, b):
                                if ti % 2 == 0:
                                    S.op("dve", lambda: nc.vector.tensor_copy(out=UF[:, c0:c0 + n], in_=PS[b][:, 0:n]), reads=[("ps", b)], writes=[("UF", ti)])
                                else:
                                    S.op("act", lambda: nc.scalar.activation(out=UF[:, c0:c0 + n], in_=PS[b][:, 0:n], func=AF.Identity),
                                         reads=[("ps", b)], writes=[("UF", ti)])
                            slab_proj(WUF[bi], ("WUF", bi), cons_uf)
                            ufk = [("UF", ti) for ti in range(NTI)]
                            b = bank()
                            for tc in range(2):
                                mm(b, 256, [(UF[:, C0 + 128 * tc:C0 + 128 * tc + 128], CS128[:])], reads=ufk + ["CS128"], c_off=256 * tc)
                            S.op("dve", lambda b=b, g=g: nc.vector.tensor_copy(out=PQc[:, :, g, :], in_=PS[b][:].rearrange("p (a c) -> p a c", c=256)),
                                 reads=[("ps", b)], writes=["PQc"])
                            ev = 0
                            for m in range(4):
                                for cp in range(4):
                                    b = bank()
                                    for h in range(2):
                                        ch = 2 * cp + h
                                        mm(b, 256, [(UF[:, L0 + 1024 * q + 128 * ch:L0 + 1024 * q + 128 * ch + 128], CS4[:, m, q, :]) for q in range(4)],
                                           reads=ufk + ["CS4"], c_off=256 * h)
                                    dst = ZPQ[:, m, 2 * cp:2 * cp + 2, g, :]
                                    src = PS[b][:].rearrange("p (a c) -> p a c", c=256)
                                    if ev % 2 == 0:
                                        S.op("dve", lambda dst=dst, src=src: nc.vector.tensor_copy(out=dst, in_=src), reads=[("ps", b)], writes=["ZPQ"])
                                    else:
                                        S.op("act", lambda dst=dst, src=src: nc.scalar.activation(out=dst, in_=src, func=AF.Identity), reads=[("ps", b)], writes=["ZPQ"])
                                    ev += 1
                        S.barrier()

                    with ExitStack() as sf2:
                        FBall = sb(sf2, "fball", [128, 4, 2048], BF16)
                        FBc = sb(sf2, "fbc", [128, 4, 256], BF16)
                        CB = [sb(sf2, "cb%d" % i, [128, 4, 512], BF16) for i in range(2)]
                        SBf = [sb(sf2, "sbf%d" % i, [128, 4, 512], BF16) for i in range(2)]
                        b4 = [0]

                        def bank4():
                            b = 4 + b4[0] % 4
                            b4[0] += 1
                            return b

                        def epilogue(g, c0, n, fb_ap, fbkey, ti):
                            e = (g + ti) % 2
                            by = bank4()
                            mm(by, n, [(FW[:, g, :], fb_ap)], reads=["FW"] + fbkey)
                            bz = bank4()
                            mm(bz, n, [(WZF[:, g, kc, :], HT[:, kc, c0:c0 + n]) for kc in range(8)], reads=[("WZF", g), ("HT", ti)])
                            S.op("act", lambda: nc.scalar.activation(out=ZS[e][:, 0:n], in_=PS[bz][:, 0:n], func=AF.Silu), reads=[("ps", bz)], writes=[("ZSF", e)])
                            S.op("dve", lambda: nc.vector.tensor_tensor(out=GS[e][:, 0:n], in0=PS[by][:, 0:n], in1=ZS[e][:, 0:n], op=ALU.mult),
                                 reads=[("ps", by), ("ZSF", e)], writes=[("GS", e)])
                            S.dma("sp", GT[l][128 * (8 + g):128 * (8 + g) + 128, c0:c0 + n], GS[e][:, 0:n], reads=[("GS", e)], writes=[("GT", 8 + g, ti)], key=("gs", e))

                        for g in range(4):
                            pairs = []
                            for tc in range(2):
                                pairs.append((PQc[:, tc, g, 0:128], CS256[:, 0, tc, :]))
                                pairs.append((PQc[:, tc, g, 128:256], CS256[:, 1, tc, :]))
                            mm(g, 256, pairs, reads=["PQc", "CS256"])
                            S.op("act", lambda g=g: nc.scalar.activation(out=FBc[:, g, :], in_=PS[g][:, 0:256], func=AF.Identity),
                                 reads=[("ps", g)], writes=[("FBc", g)])
                        for g in range(4):
                            epilogue(g, C0, 256, FBc[:, g, :], [("FBc", g)], 0)
                        ring = [0]
                        blocks = [(kp, m, hf) for kp in range(2) for m in range(4) for hf in range(2)]

                        def issue_dft(i):
                            kp, m, hf = blocks[i]
                            r = i % 2
                            S.dma("sp", CB[r][:], dft_d[m, kp, 0][:, 4 * hf:4 * hf + 4, :], writes=[("CB", r)], key=("cb", r))
                            S.dma("sp", SBf[r][:], dft_d[m, kp, 1][:, 4 * hf:4 * hf + 4, :], writes=[("SBf", r)], key=("sbf", r))
                        issue_dft(0)
                        for i, (kp, m, hf) in enumerate(blocks):
                            r = i % 2
                            if i + 1 < len(blocks):
                                issue_dft(i + 1)
                            for cl in range(4):
                                ch = 4 * hf + cl
                                first = (hf == 0 and cl == 0)
                                lastm = (hf == 1 and cl == 3)
                                for g in range(4):
                                    S.op("pe", lambda g=g, ch=ch, cl=cl, first=first: nc.tensor.matmul(
                                        PS[g][:, 0:512], lhsT=ZPQ[:, m, ch, g, 0:128], rhs=CB[r][:, cl, :], start=first, stop=False),
                                        reads=(["ZPQ", ("CB", r), ("SBf", r)] if cl == 0 and g == 0 else ()), writes=[("ps", g)], inc=False)
                                    S.op("pe", lambda g=g, ch=ch, cl=cl, lastm=lastm: nc.tensor.matmul(
                                        PS[g][:, 0:512], lhsT=ZPQ[:, m, ch, g, 128:256], rhs=SBf[r][:, cl, :], start=False, stop=lastm),
                                        reads=(), writes=[("ps", g)], inc=(cl == 3 and g == 3))
                            S._record((S.engs["pe"]["sem"], S.engs["pe"]["cnt"], "pe"), [("CB", r), ("SBf", r)], [])
                            if hf == 1:
                                for g in range(4):
                                    dst = FBall[:, g, :].rearrange("p (k f) -> p k f", f=4)[:, :, m]
                                    if g % 2 == 0:
                                        S.op("act", lambda g=g, dst=dst: nc.scalar.activation(out=dst, in_=PS[g][:, 0:512], func=AF.Identity),
                                             reads=[("ps", g)], writes=[("FBall", g)])
                                    else:
                                        S.op("dve", lambda g=g, dst=dst: nc.vector.tensor_copy(out=dst, in_=PS[g][:, 0:512]),
                                             reads=[("ps", g)], writes=[("FBall", g)])
                                if m == 3:
                                    for it in range(4):
                                        ti = 1 + 4 * kp + it
                                        for g in range(4):
                                            epilogue(g, L0 + 2048 * kp + 512 * it, 512, FBall[:, g, 512 * it:512 * it + 512], [("FBall", g)], ti)
                        S.barrier()
            S.barrier()

            with ExitStack() as sbk:
                PA = sb(sbk, "pa", [128, 8, 1024], BF16)
                PBt = sb(sbk, "pb", [128, 4, 1024], BF16)
                PCt = sb(sbk, "pc", [128, 4, 1024], BF16)
                WO = sb(sbk, "wo", [128, 8, 1024], BF16)
                WG = sb(sbk, "wg", [128, 8, 3072], BF16)
                XTs = [sb(sbk, "bxt%d" % i, [128, 8, 512]) for i in range(2)]
                GTts = [sb(sbk, "gtt%d" % i, [128, 16, 512], BF16) for i in range(2)]
                HTts = [sb(sbk, "htt%d" % i, [128, 8, 512], BF16) for i in range(2)]
                SQ = sb(sbk, "bsq", [128, 4, 512], BF16)
                MT = sb(sbk, "mt", [128, 8, 512], BF16)
                RS = sb(sbk, "brs", [128, 512])
                TMP = sb(sbk, "btmp", [128, 2, 512])
                SG = [sb(sbk, "sg%d" % i, [128, 512]) for i in range(2)]
                MM = [sb(sbk, "mm%d" % i, [128, 512]) for i in range(2)]
                S.dma("sp", PA[:], WBS[l][0:1024, :].rearrange("(kc p) c -> p kc c", p=128), writes=["PA"], key="wpa")
                S.dma("sp", PBt[:], WBS[l][1024:1536, :].rearrange("(kc p) c -> p kc c", p=128), writes=["PBt"], key="wpb")
                S.dma("sp", PCt[:], WBS[l][1536:2048, :].rearrange("(kc p) c -> p kc c", p=128), writes=["PCt"], key="wpc")
                for h in range(2):
                    S.dma("sp", WG[:, 4 * h:4 * h + 4, :], WGS[l][512 * h:512 * h + 512, :].rearrange("(kc p) c -> p kc c", p=128), writes=["WG"], key="wwg")
                S.dma("sp", WO[:], WBS[l][2048:3072, :].rearrange("(kc p) c -> p kc c", p=128), writes=["WO"], key="wwo")
                tiles = list(enumerate(TILES))
                if last:
                    tiles = tiles[1:]

                def prefetch_x(idx):
                    ti, (c0, n) = tiles[idx]
                    bi = idx % 2
                    S.dma("sp", XTs[bi][:, :, 0:n], XS[l].rearrange("(fc p) t -> p fc t", p=128)[:, :, c0:c0 + n],
                          writes=[("BXT", bi)], key=("bxt", bi))

                def prefetch_g(idx):
                    ti, (c0, n) = tiles[idx]
                    bi = idx % 2
                    S.dma("sp", GTts[bi][:, :, 0:n], GT[l].rearrange("(s p) t -> p s t", p=128)[:, :, c0:c0 + n], writes=[("GTt", bi)], key=("gtt", bi))

                def hstage(idx):
                    ti, (c0, n) = tiles[idx]
                    bi = idx % 2
                    s_ = 1 if ti == 0 else 0
                    rms_tile(XTs[bi], ("BXT", bi), n, lambda fc: GM[:, l, fc, s_:s_ + 1], lambda fc: MOD[:, l, fc, s_:s_ + 1],
                             lambda fc: HTts[bi][:, fc, 0:n], SQ, RS, TMP, [("HTt", bi)])

                prefetch_x(0)
                prefetch_g(0)
                hstage(0)
                if len(tiles) > 1:
                    prefetch_x(1)
                    prefetch_g(1)
                for idx, (ti, (c0, n)) in enumerate(tiles):
                    bi = idx % 2
                    XTb = XTs[bi]
                    GTt = GTts[bi]
                    HTt = HTts[bi]
                    xkey = ("BXT", bi)
                    s_ = 1 if ti == 0 else 0
                    for oc in range(8):
                        if oc == 4 and idx + 1 < len(tiles):
                            hstage(idx + 1)
                        osl = slice(128 * oc, 128 * oc + 128)
                        yb_ = []
                        for br, (Wt, wk, nk, s0_) in enumerate(((PA, "PA", 8, 0), (PBt, "PBt", 4, 8), (PCt, "PCt", 4, 12))):
                            b = bank()
                            mm(b, n, [(Wt[:, kc, osl], GTt[:, s0_ + kc, 0:n]) for kc in range(nk)], reads=[wk, ("GTt", bi)])
                            yb_.append(b)
                        for br in range(3):
                            b = bank()
                            sgi = br % 2
                            mm(b, n, [(WG[:, kc, 1024 * br + 128 * oc:1024 * br + 128 * oc + 128], HTt[:, kc, 0:n]) for kc in range(8)], reads=["WG", ("HTt", bi)])
                            S.op("act", lambda b=b, sgi=sgi: nc.scalar.activation(out=SG[sgi][:, 0:n], in_=PS[b][:, 0:n], func=AF.Sigmoid),
                                 reads=[("ps", b)], writes=[("SG", sgi)])
                            if br == 0:
                                S.op("dve", lambda: nc.vector.tensor_tensor(out=MM[0][:, 0:n], in0=PS[yb_[0]][:, 0:n], in1=SG[0][:, 0:n], op=ALU.mult),
                                     reads=[("ps", yb_[0]), ("SG", 0)], writes=[("MM", 0)])
                            elif br == 1:
                                S.op("dve", lambda: nc.vector.tensor_tensor(out=MM[1][:, 0:n], in0=PS[yb_[1]][:, 0:n], in1=SG[1][:, 0:n], op=ALU.mult),
                                     reads=[("ps", yb_[1]), ("SG", 1)], writes=[("MM", 1)])
                                S.op("pool", lambda: nc.gpsimd.tensor_tensor(out=MM[0][:, 0:n], in0=MM[0][:, 0:n], in1=MM[1][:, 0:n], op=ALU.add),
                                     reads=[("MM", 0), ("MM", 1)], writes=[("MM", 0)])
                            else:
                                S.op("dve", lambda: nc.vector.tensor_tensor(out=MM[1][:, 0:n], in0=PS[yb_[2]][:, 0:n], in1=SG[0][:, 0:n], op=ALU.mult),
                                     reads=[("ps", yb_[2]), ("SG", 0)], writes=[("MM", 1)])
                                S.op("pool", lambda oc=oc: nc.gpsimd.tensor_tensor(out=MT[:, oc, 0:n], in0=MM[0][:, 0:n], in1=MM[1][:, 0:n], op=ALU.add),
                                     reads=[("MM", 0), ("MM", 1)], writes=[("MT", oc)])
                    for o in range(8):
                        b = bank()
                        mm(b, n, [(WO[:, oc, 128 * o:128 * o + 128], MT[:, oc, 0:n]) for oc in range(8)], reads=["WO"] + [("MT", oc) for oc in range(8)])
                        S.op("dve", lambda o=o, b=b: nc.vector.scalar_tensor_tensor(out=XTb[:, o, 0:n], in0=PS[b][:, 0:n], scalar=MOD[:, l, 16 + o, s_:s_ + 1],
                                                                                    in1=XTb[:, o, 0:n], op0=ALU.mult, op1=ALU.add),
                             reads=[("ps", b), "MOD", xkey], writes=[xkey])
                    if not last:
                        S.dma("sp", XS[l + 1].rearrange("(fc p) t -> p fc t", p=128)[:, :, c0:c0 + n], XTb[:, :, 0:n],
                              reads=[xkey], writes=[("XS", ti)], key=("bxo", bi))
                    else:
                        rms_tile(XTb, xkey, n, lambda fc: VEC[:, V_FG + fc:V_FG + fc + 1], None,
                                 lambda fc: XTb[:, fc, 0:n], SQ, RS, TMP, [xkey])
                        t0 = c0 - L0
                        S.dma("sp", outT.rearrange("(fc p) t -> p fc t", p=128)[:, :, t0:t0 + n], XTb[:, :, 0:n],
                              reads=[xkey], writes=[("OT", ti)], key=("ot", bi))
                    if idx + 2 < len(tiles):
                        prefetch_x(idx + 2)
                        prefetch_g(idx + 2)
                S.barrier()
        for k, Dm in S.dsems.items():
            nc.sync.wait_ge(Dm["sem"], Dm["cnt"])
    return nc


def _consts():
    c = {}
    rows = SEQ // 64
    gr, gc = np.meshgrid(np.arange(rows), np.arange(64), indexing="ij")
    quarter = D // 4
    omega = (1.0 / (np.float32(10000.0) ** (np.arange(quarter, dtype=np.float32) / np.float32(quarter)))).astype(np.float32)

    def emb(p):
        ang = p.reshape(-1).astype(np.float32)[:, None] * omega[None, :]
        return np.concatenate([np.sin(ang), np.cos(ang)], axis=-1)
    pos = np.concatenate([emb(gr), emb(gc)], axis=-1).astype(np.float32)
    c["posT"] = np.ascontiguousarray(pos.T)
    c["ident"] = np.eye(128, dtype=np.float32)
    edge = np.ones((4, 2, 8), np.float32)
    for g in range(4):
        hw = (2 << g) // 2
        for m in range(hw):
            edge[g, 0, m] = 1.0 / (m + hw)
        for m in range(hw - 1):
            edge[g, 1, m] = 1.0 / (2 * hw - 1 - m)
    c["edge"] = np.ascontiguousarray(np.broadcast_to(edge.reshape(1, 64), (128, 64))).astype(np.float32)
    i = np.arange(128)
    ang = 2.0 * np.pi * ((i[:, None] * i[None, :]) % 128) / 128.0
    c["cs128"] = (np.concatenate([np.cos(ang), np.sin(ang)], axis=1) / np.sqrt(128.0)).astype(ml_dtypes.bfloat16)
    t = np.arange(256)
    a2 = 2.0 * np.pi * ((t[:, None] * t[None, :]) % 256) / 256.0
    cs = np.stack([np.cos(a2), -np.sin(a2)], axis=0) / 16.0
    c["cs256"] = np.ascontiguousarray(cs.reshape(2, 2, 128, 256).transpose(2, 0, 1, 3)).astype(ml_dtypes.bfloat16)
    tab = 2.0 * np.pi * np.arange(4096) / 4096.0
    ct = (np.cos(tab) / 64.0).astype(np.float32)
    sn = (-np.sin(tab) / 64.0).astype(np.float32)
    tp = np.arange(1024, dtype=np.int64)
    dft = np.empty((4, 2, 2, 128, 8, 512), dtype=ml_dtypes.bfloat16)
    for m in range(4):
        kk = 4 * np.arange(1024, dtype=np.int64) + m
        idx = (tp[:, None] * kk[None, :]) % 4096
        for cs_i, tb in enumerate((ct, sn)):
            mt = tb[idx].astype(ml_dtypes.bfloat16).reshape(8, 128, 2, 512)
            dft[m, :, cs_i] = mt.transpose(2, 1, 0, 3)
    c["dft"] = dft
    cm = np.cos(ang) / np.sqrt(128.0)
    sm = np.sin(ang) / np.sqrt(128.0)
    cs4 = np.empty((128, 4, 4, 256), np.float32)
    for m in range(4):
        for q in range(4):
            cq = float(np.round(np.cos(2.0 * np.pi * m * q / 4.0)))
            sq = float(np.round(np.sin(2.0 * np.pi * m * q / 4.0)))
            cs4[:, m, q, 0:128] = cq * cm - sq * sm
            cs4[:, m, q, 128:256] = cq * sm + sq * cm
    c["cs4"] = cs4.astype(ml_dtypes.bfloat16)
    return c


def _vecs(inp):
    v = np.zeros((128, NV), np.float32)

    def put(col, arr):
        a = np.asarray(arr, np.float32).reshape(-1, 128)
        v[:, col:col + a.shape[0]] = a.T
    for l in range(DEPTH):
        o = l * V_L
        put(o + V_NG, inp["norm_g"][l])
        put(o + V_AB, inp["ada_b"][l])
        put(o + V_CW, inp["conv_w"][l])
        put(o + V_CB, inp["conv_b"][l])
        put(o + V_BA, inp["lru_ba"][l])
        put(o + V_BX, inp["lru_bx"][l])
        put(o + V_LAM, inp["lru_lam"][l])
        put(o + V_PS, inp["pool_scale"][l])
    put(V_FG, inp["final_g"])
    return v


def _blockdiag(inp):
    bd = np.zeros((DEPTH, 8, 128, 4, 128), np.float32)
    for l in range(DEPTH):
        for d in range(2):
            for gi, nm in enumerate(("lru_wa", "lru_wx")):
                w = np.asarray(inp[nm][l, d], np.float32)
                for j in range(8):
                    for hh in range(2):
                        bd[l, j, 64 * hh:64 * hh + 64, 2 * d + gi, 64 * hh:64 * hh + 64] = w[2 * j + hh]
    return bd


def _in_maps(inp, cores):
    c = _consts()
    f = lambda k: np.ascontiguousarray(np.asarray(inp[k], np.float32))
    shared = dict(c)
    shared["vecs"] = _vecs(inp)
    shared["bd"] = _blockdiag(inp)
    for k in ("ada_w", "w_in", "fft_w", "pool_w", "proj_a", "proj_b", "proj_c", "w_out"):
        shared[k] = f(k)
    cctx = np.asarray(inp["c_ctx"], np.float32).reshape(8, 128).T
    maps = []
    for b in cores:
        m = dict(shared)
        m["xT"] = np.ascontiguousarray(np.asarray(inp["x"][b], np.float32).T)
        m["ctxT"] = np.ascontiguousarray(np.asarray(inp["ctx"][b], np.float32).T)
        cb = np.asarray(inp["c"][b], np.float32).reshape(8, 128).T
        m["cc"] = np.ascontiguousarray(np.concatenate([cb, cctx], axis=1))
        maps.append(m)
    return maps


def kernel(**inputs):
    nb = inputs["x"].shape[0]
    nc = build_program()
    maps = _in_maps(inputs, list(range(nb)))
    res = run_bass_kernel_spmd(nc, maps, core_ids=list(range(nb)))
    out = np.stack([np.ascontiguousarray(r["outT"].T) for r in res.results]).astype(np.float32)
    return out
```
